# Optimizing a Trainium2 kernel written in Bass

```python
import jax, jax.numpy as jnp
from jax import lax
import numpy as np

D_MODEL = 2048
BATCH = 2
SEQ = 4096
DEPTH = 1

CHUNK = 64
QBLOCK = 128
FOX_HEADS = 16
FOX_HEAD_DIM = 64
FOX_WIDTH = FOX_HEADS * FOX_HEAD_DIM
MLA_HEADS = 8
MLA_NOPE_DIM = 128
MLA_ROPE_DIM = 64
MLA_V_DIM = 128
MLA_Q_LORA = 512
MLA_KV_LORA = 256
MLA_QK_DIM = MLA_NOPE_DIM + MLA_ROPE_DIM
MLA_WIDTH = MLA_HEADS * MLA_V_DIM
ROPE_THETA = 10000.0
D_FF = 4 * D_MODEL
LN_EPS = 1e-5
RMS_EPS = 1e-6
N_ADA = 6
IN_SPLITS = (3 * FOX_WIDTH, FOX_HEADS, MLA_Q_LORA, MLA_KV_LORA, MLA_ROPE_DIM, D_MODEL, D_MODEL)
D_IN = sum(IN_SPLITS)

kernel_name = 'hybrid_fox_mla_sqrelu_block'


def layer_norm(x, g, b):
    xf = x.astype(jnp.float32)
    mu = jnp.mean(xf, axis=-1, keepdims=True)
    var = jnp.mean(jnp.square(xf - mu), axis=-1, keepdims=True)
    y = (xf - mu) * lax.rsqrt(var + LN_EPS)
    return (y * g + b).astype(x.dtype)


def rms_norm(x, g):
    xf = x.astype(jnp.float32)
    y = xf * lax.rsqrt(jnp.mean(jnp.square(xf), axis=-1, keepdims=True) + RMS_EPS)
    return (y * g).astype(x.dtype)


def rope_tables(seq_len):
    pos = jnp.arange(seq_len, dtype=jnp.float32)
    inv_freq = ROPE_THETA ** (-jnp.arange(0, MLA_ROPE_DIM, 2, dtype=jnp.float32) / MLA_ROPE_DIM)
    ang = pos[:, None] * inv_freq[None, :]
    return jnp.cos(ang), jnp.sin(ang)


def apply_rope(x, cos, sin):
    half = x.shape[-1] // 2
    x1, x2 = x[..., :half], x[..., half:]
    cos = cos.astype(x.dtype)
    sin = sin.astype(x.dtype)
    return jnp.concatenate([x1 * cos - x2 * sin, x1 * sin + x2 * cos], axis=-1)


def to_blocks(t):
    b, s = t.shape[0], t.shape[1]
    return t.reshape(b, s // QBLOCK, QBLOCK, *t.shape[2:]).swapaxes(0, 1)


def from_blocks(t):
    t = t.swapaxes(0, 1)
    return t.reshape(t.shape[0], t.shape[1] * t.shape[2], *t.shape[3:])


def fox_attention(q, k, v, log_f):
    b, s, h, dh = q.shape
    nb = s // QBLOCK
    scale = dh ** -0.5
    cum = jnp.cumsum(log_f, axis=1)
    cum_k = cum.transpose(0, 2, 1)
    k_pos = jnp.arange(s)

    def one_block(args):
        i, q_i, c_i = args
        logits = jnp.einsum('bqhd,bkhd->bhqk', q_i, k, preferred_element_type=jnp.float32) * scale
        logits = logits + (c_i.transpose(0, 2, 1)[..., :, None] - cum_k[..., None, :])
        q_pos = i * QBLOCK + jnp.arange(QBLOCK)
        mask = k_pos[None, :] <= q_pos[:, None]
        logits = jnp.where(mask, logits, -jnp.inf)
        p = jax.nn.softmax(logits, axis=-1).astype(v.dtype)
        return jnp.einsum('bhqk,bkhd->bqhd', p, v)

    out = lax.map(one_block, (jnp.arange(nb), to_blocks(q), to_blocks(cum)))
    return from_blocks(out)


def mla_attention(q_nope, q_rope, k_nope, k_rope, v):
    b, s, h, _ = q_nope.shape
    nb = s // QBLOCK
    scale = MLA_QK_DIM ** -0.5
    k_chunk = jnp.arange(s) // CHUNK

    def one_block(args):
        i, qn, qr = args
        logits = (jnp.einsum('bqhd,bkhd->bhqk', qn, k_nope, preferred_element_type=jnp.float32)
                  + jnp.einsum('bqhr,bkr->bhqk', qr, k_rope, preferred_element_type=jnp.float32)) * scale
        q_chunk = (i * QBLOCK + jnp.arange(QBLOCK)) // CHUNK
        mask = k_chunk[None, :] <= q_chunk[:, None]
        logits = jnp.where(mask, logits, -jnp.inf)
        p = jax.nn.softmax(logits, axis=-1).astype(v.dtype)
        return jnp.einsum('bhqk,bkhd->bqhd', p, v)

    out = lax.map(one_block, (jnp.arange(nb), to_blocks(q_nope), to_blocks(q_rope)))
    return from_blocks(out)


def token_mixers(u, w_in, b_forget, g_q_norm, w_q_up, g_kv_norm, w_kv_up, w_branch_fox, w_branch_mla, w_out):
    b, s, _ = u.shape
    proj = u @ w_in
    cuts = [int(v) for v in np.cumsum(IN_SPLITS)[:-1]]
    qkv, f_logit, c_q, c_kv, k_rope, g_fox, g_mla = jnp.split(proj, cuts, axis=-1)

    qkv = qkv.reshape(b, s, 3, FOX_HEADS, FOX_HEAD_DIM)
    log_f = jax.nn.log_sigmoid(f_logit.astype(jnp.float32) + b_forget.astype(jnp.float32))
    y_fox = fox_attention(qkv[:, :, 0], qkv[:, :, 1], qkv[:, :, 2], log_f).reshape(b, s, FOX_WIDTH)

    q = (rms_norm(c_q, g_q_norm) @ w_q_up).reshape(b, s, MLA_HEADS, MLA_QK_DIM)
    kv = (rms_norm(c_kv, g_kv_norm) @ w_kv_up).reshape(b, s, MLA_HEADS, MLA_NOPE_DIM + MLA_V_DIM)
    cos, sin = rope_tables(s)
    q_nope = q[..., :MLA_NOPE_DIM]
    q_rope = apply_rope(q[..., MLA_NOPE_DIM:], cos[:, None, :], sin[:, None, :])
    k_rope = apply_rope(k_rope, cos, sin)
    k_nope = kv[..., :MLA_NOPE_DIM]
    v = kv[..., MLA_NOPE_DIM:]
    y_mla = mla_attention(q_nope, q_rope, k_nope, k_rope, v).reshape(b, s, MLA_WIDTH)

    merged = jax.nn.sigmoid(g_fox) * (y_fox @ w_branch_fox) + jax.nn.sigmoid(g_mla) * (y_mla @ w_branch_mla)
    return merged @ w_out


def setup_inputs(seed: int = 0) -> dict:
    key = jax.random.key(seed)
    ks = jax.random.split(key, 24)
    beta = (8.0 * DEPTH) ** -0.25
    f32 = jnp.float32

    def nrm(k, shape, fan_in, mult=1.0):
        return jax.random.normal(k, shape, f32) * (fan_in ** -0.5) * mult

    def gain(k, shape):
        return 1.0 + 0.02 * jax.random.normal(k, shape, f32)

    def bias(k, shape):
        return 0.02 * jax.random.normal(k, shape, f32)

    return {
        'x': jax.random.normal(ks[0], (BATCH, SEQ, D_MODEL), f32),
        'c': jax.random.normal(ks[1], (BATCH, D_MODEL), f32),
        'w_ada': nrm(ks[2], (DEPTH, D_MODEL, N_ADA * D_MODEL), D_MODEL),
        'b_ada': bias(ks[3], (DEPTH, N_ADA * D_MODEL)),
        'w_in': nrm(ks[4], (DEPTH, D_MODEL, D_IN), D_MODEL),
        'b_forget': jax.random.uniform(ks[5], (DEPTH, FOX_HEADS), f32, minval=1.0, maxval=5.0),
        'g_q_norm': gain(ks[6], (DEPTH, MLA_Q_LORA)),
        'w_q_up': nrm(ks[7], (DEPTH, MLA_Q_LORA, MLA_HEADS * MLA_QK_DIM), MLA_Q_LORA),
        'g_kv_norm': gain(ks[8], (DEPTH, MLA_KV_LORA)),
        'w_kv_up': nrm(ks[9], (DEPTH, MLA_KV_LORA, MLA_HEADS * (MLA_NOPE_DIM + MLA_V_DIM)), MLA_KV_LORA),
        'w_branch_fox': nrm(ks[10], (DEPTH, FOX_WIDTH, D_MODEL), FOX_WIDTH),
        'w_branch_mla': nrm(ks[11], (DEPTH, MLA_WIDTH, D_MODEL), MLA_WIDTH),
        'w_out': nrm(ks[12], (DEPTH, D_MODEL, D_MODEL), D_MODEL, beta),
        'ln1_g': gain(ks[13], (DEPTH, D_MODEL)),
        'ln1_b': bias(ks[14], (DEPTH, D_MODEL)),
        'w_mlp_up': nrm(ks[15], (DEPTH, D_MODEL, D_FF), D_MODEL),
        'w_mlp_down': nrm(ks[16], (DEPTH, D_FF, D_MODEL), D_FF, beta),
        'ln2_g': gain(ks[17], (DEPTH, D_MODEL)),
        'ln2_b': bias(ks[18], (DEPTH, D_MODEL)),
    }


def reference(x, c, w_ada, b_ada, w_in, b_forget, g_q_norm, w_q_up, g_kv_norm, w_kv_up,
              w_branch_fox, w_branch_mla, w_out, ln1_g, ln1_b, w_mlp_up, w_mlp_down, ln2_g, ln2_b):
    alpha = (2.0 * DEPTH) ** 0.25
    for l in range(DEPTH):
        mod = jax.nn.silu(c) @ w_ada[l] + b_ada[l]
        shift1, scale1, gate1, shift2, scale2, gate2 = jnp.split(mod[:, None, :], N_ADA, axis=-1)

        u = x * (1.0 + scale1) + shift1
        mix = token_mixers(u, w_in[l], b_forget[l], g_q_norm[l], w_q_up[l], g_kv_norm[l], w_kv_up[l],
                           w_branch_fox[l], w_branch_mla[l], w_out[l])
        x = layer_norm(alpha * x + gate1 * mix, ln1_g[l], ln1_b[l])

        u2 = x * (1.0 + scale2) + shift2
        h = jnp.square(jax.nn.relu(u2 @ w_mlp_up[l])) @ w_mlp_down[l]
        x = layer_norm(alpha * x + gate2 * h, ln2_g[l], ln2_b[l])
    return x
```

```python
import numpy as np
from contextlib import ExitStack
import concourse.bass as bass
import concourse.mybir as mybir
from concourse.bass_utils import run_bass_kernel_spmd

F32 = mybir.dt.float32
BF16 = mybir.dt.bfloat16
ALU = mybir.AluOpType
AF = mybir.ActivationFunctionType

D = 2048
S = 4096
T = 1024
KC = 16
NEG = -30000.0
ALPHA = 2.0 ** 0.25
LN_EPS = 1e-5
RMS_EPS = 1e-6
ENGS = ("pe", "act", "dve", "pool", "sp")
GR = 256
DEBUG_SRC = False
SEG_PE_OPS = 2500


class Op:
    __slots__ = ("eng", "fn", "deps", "dkey", "dval", "sig", "sigval", "idx", "src")


class Prog:
    def __init__(self, nc):
        self.nc = nc
        self.ops = []
        self.w = {}
        self.r = {}
        self.dcount = {}

    def op(self, eng, fn, R=(), W=(), dkey=None, join=False):
        o = Op()
        o.eng, o.fn, o.dkey, o.idx = eng, fn, dkey, len(self.ops)
        o.sig, o.sigval, o.dval = False, 0, 0
        o.src = ""
        if DEBUG_SRC:
            import sys as _s
            fr = _s._getframe(1)
            ln = []
            while fr is not None and len(ln) < 4:
                ln.append(str(fr.f_lineno))
                fr = fr.f_back
            o.src = ",".join(ln)
        deps = set()
        ops = self.ops
        for k in R:
            ws = self.w.get(k)
            if ws:
                deps |= ws
        for k in W:
            ws = self.w.get(k)
            rs = self.r.get(k)
            jn = (join and dkey is not None and ws and not rs and all(ops[i].dkey == dkey for i in ws))
            if jn:
                ws.add(o.idx)
            else:
                if ws:
                    deps |= ws
                if rs:
                    deps |= rs
                self.w[k] = {o.idx}
                self.r[k] = set()
        for k in R:
            self.r.setdefault(k, set()).add(o.idx)
        deps.discard(o.idx)
        o.deps = deps
        if dkey is not None:
            self.dcount[dkey] = self.dcount.get(dkey, 0) + 16
            o.dval = self.dcount[dkey]
        ops.append(o)
        return o

    def emit(self, stack, final_wait_keys):
        nc = self.nc
        ops = self.ops
        for o in ops:
            for d in o.deps:
                do = ops[d]
                if do.dkey is None and not (do.eng == "pe" and o.eng == "pe"):
                    do.sig = True
        cnt = {e: 0 for e in ENGS}
        for o in ops:
            if o.dkey is None and o.sig:
                cnt[o.eng] += 1
                o.sigval = cnt[o.eng]
        esem = {e: stack.enter_context(nc.semaphore("s_" + e)) for e in ENGS}
        dsem = {k: stack.enter_context(nc.semaphore("d_%d" % i)) for i, k in enumerate(self.dcount)}
        import bisect
        klist = {}
        for o in ops:
            if o.dkey is not None:
                klist.setdefault(o.dkey, ([], []))
                klist[o.dkey][0].append(o.idx)
                klist[o.dkey][1].append(o.dval)
        bounds = [0]
        npe = 0
        for o in ops:
            if o.eng == "pe":
                npe += 1
            if npe >= SEG_PE_OPS:
                bounds.append(o.idx + 1)
                npe = 0
        if bounds[-1] != len(ops):
            bounds.append(len(ops))
        waited = {e: {} for e in ENGS}
        nseg = len(bounds) - 1
        for si in range(nseg):
            lo, hi = bounds[si], bounds[si + 1]
            per = {e: [o for o in ops[lo:hi] if o.eng == e] for e in ENGS}
            last = (si == nseg - 1)

            def run(engname, eng, per=per, last=last):
                wd = waited[engname]
                for o in per[engname]:
                    need = {}
                    for d in o.deps:
                        do = ops[d]
                        if do.dkey is not None:
                            idxs, vals = klist[do.dkey]
                            key, val = ("d", do.dkey), vals[bisect.bisect_left(idxs, o.idx) - 1]
                        else:
                            if do.eng == "pe" and engname == "pe":
                                continue
                            key, val = ("e", do.eng), do.sigval
                        if val > need.get(key, 0):
                            need[key] = val
                    for key, val in need.items():
                        if val > wd.get(key, 0):
                            wd[key] = val
                            sem = dsem[key[1]] if key[0] == "d" else esem[key[1]]
                            eng.wait_ge(sem, val)
                    ins = o.fn(eng)
                    if DEBUG_SRC:
                        ins.annotate("op%d@%s" % (o.idx, o.src))
                    if o.dkey is not None:
                        ins.then_inc(dsem[o.dkey], 16)
                    elif o.sig:
                        ins.then_inc(esem[engname], 1)
                if engname == "sp" and last:
                    for k in final_wait_keys:
                        eng.wait_ge(dsem[k], self.dcount[k])

            with nc.Block() as block:
                block.tensor(lambda e: run("pe", e))
                block.scalar(lambda e: run("act", e))
                block.vector(lambda e: run("dve", e))
                block.gpsimd(lambda e: run("pool", e))
                block.sync(lambda e: run("sp", e))


class Buf:
    def __init__(self, arena_ap, off, cols, dt):
        self.off, self.cols, self.dt = off, cols, dt
        self.esz = 2 if dt == BF16 else 4
        e0 = off // 2
        if dt == BF16:
            self.ap = arena_ap[:, e0:e0 + cols]
        else:
            self.ap = arena_ap[:, e0:e0 + 2 * cols].bitcast(F32)

    def rg(self, c0=0, c1=None):
        if c1 is None:
            c1 = self.cols
        b0 = self.off + c0 * self.esz
        b1 = self.off + c1 * self.esz
        return [("a", g) for g in range(b0 // GR, (b1 + GR - 1) // GR)]


ARENA_BYTES = 204 * 1024


def build_program(debug=False, stop=99):
    nc = bass.Bass("TRN2", target_bir_lowering=False)
    st = ExitStack()
    P = Prog(nc)

    def dram(name, shape, kind="ExternalInput", dt=F32):
        return nc.dram_tensor(name, list(shape), dt, kind=kind).ap()

    xT_seq = dram("xT_seq", [D, S])
    xT_own = dram("xT_own", [D, T])
    x_own = dram("x_own", [T, D])
    c_col = dram("c_col", [128, KC])
    w_ada = dram("w_ada", [48 * 128, 4096])
    b_ada = dram("b_ada", [1, 6 * D])
    wq = dram("wq", [D, 1024])
    wk = dram("wk", [D, 1024])
    wv = dram("wv", [D, 1024])
    wf_r = dram("wf_r", [128, 256])
    wcq = dram("wcq", [D, 512])
    wckr = dram("wckr", [D, 384])
    wgf = dram("wgf", [D, D])
    wgm = dram("wgm", [D, D])
    bfg = dram("bfg", [4, 4])
    gq = dram("gq", [128, 4])
    gkv = dram("gkv", [128, 2])
    wqn = dram("wqn", [512, 1024])
    wqr = dram("wqr", [512, 1024])
    wkn = dram("wkn", [256, 1024])
    wvm = dram("wvm", [256, 1024])
    wbf = dram("wbf", [1024, D])
    wbm = dram("wbm", [1024, D])
    wout = dram("wout", [D, D])
    lnp = dram("lnp", [4, D])
    lnc = dram("lnc", [128, 64])
    wup = dram("wup", [D, 4 * D])
    wdn = dram("wdn", [4 * D, D])
    ropeK = dram("ropeK", [128, S])
    ropeQ = dram("ropeQ", [128, T])
    maskF = dram("maskF", [128, 4 * 512])
    maskM = dram("maskM", [128, 4 * 512])
    selw = dram("selw", [128, 4])
    ident_d = dram("ident", [128, 128])
    out_d = dram("out", [T, D], kind="ExternalOutput")

    arena_t = st.enter_context(nc.sbuf_tensor("arena", [128, ARENA_BYTES // 2], BF16))
    arena_ap = arena_t[:, :]
    top = [0]

    def alloc(cols, dt):
        esz = 2 if dt == BF16 else 4
        off = top[0]
        top[0] += (cols * esz + GR - 1) // GR * GR
        assert top[0] <= ARENA_BYTES, ("arena overflow", top[0])
        return Buf(arena_ap, off, cols, dt)

    PSW = st.enter_context(nc.psum_tensor("psw", [128, 4096], F32))
    PS = [PSW[:, i * 512:(i + 1) * 512] for i in range(8)]

    def psr(b):
        return [("ps", b)]

    def act(out, in_, func, R, W, bias=None, scale=None):
        kw = {}
        if bias is not None:
            kw["bias"] = bias
        if scale is not None:
            kw["scale"] = scale
        return P.op("act", lambda e: e.activation(out=out, in_=in_, func=func, **kw), R=R, W=W)

    def ts(eng, out, in0, s1, s2, op0, op1, R, W):
        if s2 is None:
            return P.op(eng, lambda e: e.tensor_scalar(out=out, in0=in0, scalar1=s1, scalar2=None, op0=op0), R=R, W=W)
        return P.op(eng, lambda e: e.tensor_scalar(out=out, in0=in0, scalar1=s1, scalar2=s2, op0=op0, op1=op1), R=R, W=W)

    def tt(eng, out, in0, in1, op, R, W):
        return P.op(eng, lambda e: e.tensor_tensor(out=out, in0=in0, in1=in1, op=op), R=R, W=W)

    def stt(eng, out, in0, scalar, in1, op0, op1, R, W):
        return P.op(eng, lambda e: e.scalar_tensor_tensor(out=out, in0=in0, scalar=scalar, in1=in1, op0=op0, op1=op1), R=R, W=W)

    def cp(eng, out, in_, R, W):
        if eng == "act":
            return act(out, in_, AF.Copy, R, W)
        return P.op(eng, lambda e: e.tensor_copy(out=out, in_=in_), R=R, W=W)

    def memset(eng, ap, val, W):
        return P.op(eng, lambda e: e.memset(ap, val), W=W)

    def dma(q, out, in_, R, W, dkey, join=False):
        return P.op(q, lambda e: e.dma_start(out=out, in_=in_), R=R, W=W, dkey=dkey, join=join)

    def mm(out_ap, pairs, R, W, start=True, skip=True):
        def fn(e):
            n = len(pairs)
            ins = None
            for q, (l, r) in enumerate(pairs):
                ins = e.matmul(out_ap, lhsT=l, rhs=r, start=(start and q == 0), stop=(q == n - 1),
                               skip_group_check=skip)
            return ins
        return P.op("pe", fn, R=R, W=W)

    def early(k):
        if stop != k:
            return False
        import os as _os
        npad = int(_os.environ.get("PADMM", "0"))
        for q in range(npad):
            bq = q % 8
            mm(PS[bq][:, 0:128], [(ident.ap[:, :], ident.ap[:, :])], ident.rg(), psr(bq))
        dma("sp", out_d[0:128, 0:128], identf.ap, identf.rg(), [], "out")
        P.emit(st, ["out"])
        st.close()
        return True

    ident = alloc(128, BF16)
    identf = alloc(128, F32)
    ones_bf = alloc(128, BF16)
    ones_f = alloc(128, F32)
    mF = alloc(2048, BF16)
    mM = alloc(2048, BF16)
    selw_sb = alloc(4, F32)
    bfg_raw = alloc(4, F32)
    bfg_sb = alloc(4, F32)
    gq_sb = alloc(4, F32)
    gkv_sb = alloc(2, F32)
    ccol = alloc(KC, F32)
    scT = alloc(KC, BF16)
    modT = alloc(96, F32)
    lnc_sb = alloc(64, F32)
    g2c = alloc(16, F32)
    b2c = alloc(16, F32)
    wfs = alloc(256, BF16)
    NW = 3
    wring = [alloc(4096, BF16) for _ in range(NW)]
    wctr = [0]

    def wslot():
        i = wctr[0] % len(wring)
        wctr[0] += 1
        return i

    def wload(src_ap, nk, ncols, slot=None, col0=0):
        i = wslot() if slot is None else slot
        assert col0 + nk * ncols <= 4096
        wb = wring[i]
        view = wb.ap[:, col0:col0 + nk * ncols].rearrange("p (k n) -> p k n", k=nk)
        srcv = src_ap.rearrange("(k p) n -> p k n", p=128)
        reg = wb.rg(col0, col0 + nk * ncols)
        dma("pool", view, srcv, [], reg, ("w", id(wb)), join=(col0 != 0))
        return view, reg

    dma("sp", ccol.ap, c_col, [], ccol.rg(), "k0")
    dma("sp", selw_sb.ap, selw, [], selw_sb.rg(), "k1")
    dma("sp", bfg_raw.ap[0:4, :], bfg, [], bfg_raw.rg(), "k2")
    ts("dve", bfg_sb.ap[0:4, :], bfg_raw.ap[0:4, :], -1.0, None, ALU.mult, None, bfg_raw.rg(), bfg_sb.rg())
    dma("sp", gq_sb.ap, gq, [], gq_sb.rg(), "k3")
    dma("sp", gkv_sb.ap, gkv, [], gkv_sb.rg(), "k4")
    dma("sp", identf.ap, ident_d, [], identf.rg(), "k5")
    dma("sp", lnc_sb.ap, lnc, [], lnc_sb.rg(), "k10")
    dma("pool", ident.ap, ident_d, [], ident.rg(), "k6")
    dma("pool", mF.ap, maskF, [], mF.rg(), "k7")
    dma("pool", mM.ap, maskM, [], mM.rg(), "k8")
    dma("pool", wfs.ap, wf_r, [], wfs.rg(), "k9")
    memset("dve", ones_bf.ap, 1.0, ones_bf.rg())
    memset("dve", ones_f.ap, 1.0, ones_f.rg())

    psc = [0]

    def nextps():
        i = psc[0] % 8
        psc[0] += 1
        return i

    if early(0):
        return nc
    mark0 = top[0]
    brow = [alloc(256, F32) for _ in range(2)]
    mrow = [alloc(256, F32) for _ in range(2)]
    act(scT.ap, ccol.ap, AF.Silu, ccol.rg(), scT.rg())
    def ada_slab(s, brow, mrow, loader):
        wv_, wreg = loader(w_ada[s * 128:(s + 1) * 128, :])
        br, mr = brow[s % 2], mrow[s % 2]
        dma("sp", br.ap[0:1, :], b_ada[:, s * 256:(s + 1) * 256], [], br.rg(), ("brow", s % 2))
        b = nextps_fn[0]()
        mm(PS[b][0:1, 0:256], [(scT.ap[:, k:k + 1], wv_[:, k, :]) for k in range(KC)],
           scT.rg() + wreg, psr(b))
        tt("dve", mr.ap[0:1, :], PS[b][0:1, 0:256], br.ap[0:1, :], ALU.add, psr(b) + br.rg(), mr.rg())
        b2 = nextps_fn[0]()

        def fn(e, b2=b2, mr=mr):
            ins = None
            for m in range(2):
                ins = e.matmul(PS[b2][:, m:m + 1], lhsT=mr.ap[0:1, m * 128:(m + 1) * 128], rhs=ones_f.ap[0:1, 0:1],
                               start=(m == 0), stop=(m == 1), skip_group_check=True)
            return ins
        P.op("pe", fn, R=mr.rg() + ones_f.rg(), W=psr(b2))
        grp = s // 8
        if grp in (1, 4):
            ts("dve", modT.ap[:, 2 * s:2 * s + 2], PS[b2][:, 0:2], 1.0, None, ALU.add, None, psr(b2), modT.rg(2 * s, 2 * s + 2))
        else:
            cp("dve", modT.ap[:, 2 * s:2 * s + 2], PS[b2][:, 0:2], psr(b2), modT.rg(2 * s, 2 * s + 2))

    def wload_tiled(src2d):
        i = wslot()
        wb = wring[i]
        dma("pool", wb.ap[:, 0:4096], src2d, [], wb.rg(), ("w", id(wb)))
        return wb.ap[:, 0:4096].rearrange("p (k n) -> p k n", k=KC), wb.rg()

    nextps_fn = [nextps]
    for s in range(16):
        ada_slab(s, brow, mrow, wload_tiled)
    top[0] = mark0
    SH1, SC1, G1, SH2, SC2, G2 = 0, 16, 32, 48, 64, 80

    if early(1):
        return nc

    def bcast_rows(dst, c0, eng_copy="dve"):
        dg = alloc(512, F32)
        for kq in range(4):
            for kk in range(4):
                k = kq * 4 + kk
                ts("dve", dg.ap[:, kk * 128:(kk + 1) * 128], identf.ap, modT.ap[:, c0 + k:c0 + k + 1], None, ALU.mult, None,
                   identf.rg() + modT.rg(c0 + k, c0 + k + 1), dg.rg(kk * 128, (kk + 1) * 128))
            b = nextps()
            mm(PS[b][:, :], [(ones_f.ap[:, :], dg.ap[:, :])], ones_f.rg() + dg.rg(), psr(b))
            cp(eng_copy, dst.ap[:, kq * 512:(kq + 1) * 512], PS[b][:, :], psr(b), dst.rg(kq * 512, (kq + 1) * 512))

    def make_stream(nx=3, nu=2):
        xs = [alloc(1024, F32) for _ in range(nx)]
        uts = [alloc(KC * 512, BF16) for _ in range(nu)]
        return dict(xs=xs, uts=uts, xc=0, uc=0)

    def stream_uT(stq, src, t, dst=None, dcol=0, eng="dve"):
        if dst is None:
            ub = stq["uts"][stq["uc"] % len(stq["uts"])]
            stq["uc"] += 1
            base = 0
            stride = 512
        else:
            ub, base, stride = dst, dcol, 1024
        for kg in range(8):
            xb = stq["xs"][stq["xc"] % len(stq["xs"])]
            stq["xc"] += 1
            srcv = src[kg * 256:(kg + 1) * 256, t * 512:(t + 1) * 512].rearrange("(k p) n -> p k n", p=128)
            dma("sp", xb.ap.rearrange("p (k n) -> p k n", k=2), srcv, [], xb.rg(), ("xs", id(xb)))
            for kk in range(2):
                k = 2 * kg + kk
                c0 = k * stride + base
                ts(eng, ub.ap[:, c0:c0 + 512], xb.ap[:, kk * 512:(kk + 1) * 512],
                   modT.ap[:, SC1 + k:SC1 + k + 1], modT.ap[:, SH1 + k:SH1 + k + 1], ALU.mult, ALU.add,
                   xb.rg(kk * 512, (kk + 1) * 512) + modT.rg(SC1 + k, SC1 + k + 1) + modT.rg(SH1 + k, SH1 + k + 1),
                   ub.rg(c0, c0 + 512))
        return ub, base, stride

    def uT_ap(ub, base, stride, k, c0=0, n=512):
        o = k * stride + base + c0
        return ub.ap[:, o:o + n]

    def uT_rg(ub, base, stride, k, c0=0, n=512):
        o = k * stride + base + c0
        return ub.rg(o, o + n)

    PT = [alloc(1024, BF16) for _ in range(2)]
    ptc = [0]
    yb = alloc(8 * 2048, BF16)
    rec = [alloc(8, F32) for _ in range(2)]
    mark_att = top[0]

    SPAIRS = [0, 2]
    OBANKS = [4, 5]
    BBANKS = [6, 7, 0, 1, 2, 3]
    bb_n = [6]
    bctr = [0]

    def nextb():
        i = BBANKS[bctr[0] % bb_n[0]]
        bctr[0] += 1
        return i

    sctr = [0]

    def attention(nh, hd, scale, mask, k_ops, v_ap, v_rg, ycol0, after_slot=None, build_units=None):
        W = nh * 128
        G = 1024 // W
        vw = hd + 1
        pending_units = build_units(0) if build_units is not None else []
        pend = None

        def flush(pend):
            pt, g0, i0, ob0, nsteps0 = pend
            for r in range(G):
                s0 = g0 * G + r
                for hh in range(nh):
                    c0 = r * W + hh * 128

                    def fn(e, pt=pt, s0=s0, hh=hh, ob=ob0, nsteps=nsteps0, c0=c0):
                        return e.matmul(PS[ob][:, hh * vw:(hh + 1) * vw], lhsT=pt.ap[:, c0:c0 + 128],
                                        rhs=v_ap(hh, s0), start=(s0 == 0 and hh == 0), stop=(s0 == nsteps - 1),
                                        skip_group_check=True)
                    P.op("pe", fn, R=pt.rg(c0, c0 + 128) + v_rg(hh, s0), W=psr(ob0))
            if (g0 + 1) * G == nsteps0:
                rc = rec[i0 % 2]
                ov = PS[ob0][:, 0:nh * vw].rearrange("p (h d) -> p h d", d=vw)
                P.op("dve", lambda e, rc=rc, ov=ov: e.reciprocal(out=rc.ap[:, 0:nh], in_=ov[:, :, hd]), R=psr(ob0), W=rc.rg())
                for hh in range(nh):
                    c0 = i0 * 2048 + ycol0 + hh * hd
                    ts("dve", yb.ap[:, c0:c0 + hd], PS[ob0][:, hh * vw:hh * vw + hd], rc.ap[:, hh:hh + 1], None, ALU.mult, None,
                       psr(ob0) + rc.rg(), yb.rg(c0, c0 + hd))
                if after_slot is not None:
                    after_slot(i0)

        for i in range(8):
            for u in pending_units:
                u()
            pending_units = build_units(i + 1) if (build_units is not None and i + 1 < 8) else []
            nsteps = 4 * i + 4
            ngr = nsteps // G
            ob = OBANKS[i % 2]
            for g in range(ngr):
                sp0 = SPAIRS[sctr[0] % 2]
                sctr[0] += 1
                for r in range(G):
                    s = g * G + r
                    col = r * W
                    bk = sp0 + col // 512
                    cb = col % 512
                    first = True
                    if s >= 4 * i:
                        mm(PS[bk][:, cb:cb + W], [(ident.ap[:, :], mask.ap[:, (s - 4 * i) * 512:(s - 4 * i) * 512 + W])],
                           ident.rg() + mask.rg(), psr(bk), start=True)
                        first = False
                    for hh in range(nh):
                        pairs, regs = k_ops(hh, i, s)

                        def fn(e, pairs=pairs, first=first, bk=bk, c0=cb + hh * 128):
                            ins = None
                            for q, (l, r_) in enumerate(pairs):
                                ins = e.matmul(PS[bk][:, c0:c0 + 128], lhsT=l, rhs=r_,
                                               start=(first and q == 0), stop=(q == len(pairs) - 1), skip_group_check=True)
                            return ins
                        P.op("pe", fn, R=regs, W=psr(bk))
                        first = False
                pt = PT[ptc[0] % 2]
                ptc[0] += 1
                act(pt.ap[:, 0:1024], PSW[:, sp0 * 512:sp0 * 512 + 1024], AF.Exp, psr(sp0) + psr(sp0 + 1), pt.rg(0, 1024), scale=scale)
                if pending_units and g >= 1:
                    pending_units.pop(0)()
                if pend is not None:
                    flush(pend)
                pend = (pt, g, i, ob, nsteps)
        flush(pend)

    stq = make_stream()
    KT = alloc(4 * 4096, BF16)
    VA = alloc(32 * 4 * 65, BF16)
    QT = alloc(4 * 1024, BF16)
    fe = alloc(512, F32)
    fl = alloc(512, F32)
    fc = [alloc(512, F32) for _ in range(2)]
    fr1 = alloc(512, F32)
    pk = alloc(3 * 512, BF16)
    co = [alloc(128, F32) for _ in range(2)]
    qr1 = alloc(128, F32)
    pq = alloc(3 * 128, BF16)
    onesr = alloc(512, F32)
    memset("dve", onesr.ap[0:4, :], 1.0, onesr.rg())

    memset("pool", KT.ap[64:70, :], 1.0, KT.rg())
    memset("pool", QT.ap[64:70, :], 1.0, QT.rg())
    memset("pool", VA.ap[:, :], 1.0, VA.rg())
    pre_own = None
    for p in range(4):
        Wq, Wq_r = wload(wq[:, p * 256:(p + 1) * 256], KC, 256)
        Wk, Wk_r = wload(wk[:, p * 256:(p + 1) * 256], KC, 256)
        Wv, Wv_r = wload(wv[:, p * 256:(p + 1) * 256], KC, 256)
        for n in range(2):
            ub, base, stride = pre_own[n] if pre_own is not None else stream_uT(stq, xT_own, n)
            for m in range(2):
                b = nextb()
                mm(PS[b][:, :], [(Wq[:, k, m * 128:(m + 1) * 128], uT_ap(ub, base, stride, k)) for k in range(KC)],
                   Wq_r + ub.rg(), psr(b))
                for hq in range(2):
                    c0 = (2 * m + hq) * 1024 + n * 512
                    cp("act", QT.ap[0:64, c0:c0 + 512], PS[b][hq * 64:(hq + 1) * 64, :], psr(b), QT.rg(c0, c0 + 512))
        nxt = stream_uT(stq, xT_seq, 0)
        for t in range(8):
            ub, base, stride = nxt
            if t + 1 < 8:
                nxt = stream_uT(stq, xT_seq, t + 1)
            for m in range(2):
                b = nextb()
                mm(PS[b][:, :], [(Wk[:, k, m * 128:(m + 1) * 128], uT_ap(ub, base, stride, k)) for k in range(KC)],
                   Wk_r + ub.rg(), psr(b))
                for hq in range(2):
                    c0 = (2 * m + hq) * 4096 + t * 512
                    cp("act", KT.ap[0:64, c0:c0 + 512], PS[b][hq * 64:(hq + 1) * 64, :], psr(b), KT.rg(c0, c0 + 512))
            for sub in range(4):
                b = nextb()
                mm(PS[b][:, 0:256], [(uT_ap(ub, base, stride, k, sub * 128, 128), Wv[:, k, :]) for k in range(KC)],
                   Wv_r + ub.rg(), psr(b))
                c0 = (4 * t + sub) * 260
                ov = VA.ap[:, c0:c0 + 260].rearrange("p (h d) -> p h d", d=65)[:, :, 0:64]
                iv = PS[b][:, 0:256].rearrange("p (h d) -> p h d", d=64)
                cp("act", ov, iv, psr(b), VA.rg(c0, c0 + 260))
            b = nextb()
            mm(PS[b][0:4, :], [(wfs.ap[:, k * 16 + 4 * p:k * 16 + 4 * p + 4], uT_ap(ub, base, stride, k)) for k in range(KC)],
               wfs.rg() + ub.rg(), psr(b))
            act(fe.ap[0:4, :], PS[b][0:4, :], AF.Exp, psr(b) + bfg_sb.rg(), fe.rg(), bias=bfg_sb.ap[0:4, p:p + 1], scale=-1.0)
            act(fl.ap[0:4, :], fe.ap[0:4, :], AF.Ln, fe.rg(), fl.rg(), bias=1.0)
            cc, cprev = fc[t % 2], fc[(t + 1) % 2]
            if t == 0:
                P.op("dve", lambda e, cc=cc: e.tensor_tensor_scan(out=cc.ap[0:4, :], data0=onesr.ap[0:4, :], data1=fl.ap[0:4, :],
                                                                  initial=0.0, op0=ALU.mult, op1=ALU.add),
                     R=onesr.rg() + fl.rg(), W=cc.rg())
            else:
                P.op("dve", lambda e, cc=cc, cprev=cprev: e.tensor_tensor_scan(
                    out=cc.ap[0:4, :], data0=onesr.ap[0:4, :], data1=fl.ap[0:4, :],
                    initial=cprev.ap[0:4, 511:512], op0=ALU.mult, op1=ALU.add),
                    R=onesr.rg() + fl.rg() + cprev.rg(), W=cc.rg())
            ts("dve", pk.ap[0:4, 0:512], cc.ap[0:4, :], 8.0, None, ALU.mult, None, cc.rg(), pk.rg(0, 512))
            stt("dve", fr1.ap[0:4, :], cc.ap[0:4, :], 8.0, pk.ap[0:4, 0:512], ALU.mult, ALU.subtract, cc.rg() + pk.rg(0, 512), fr1.rg())
            cp("dve", pk.ap[0:4, 512:1024], fr1.ap[0:4, :], fr1.rg(), pk.rg(512, 1024))
            tt("dve", pk.ap[0:4, 1024:1536], fr1.ap[0:4, :], pk.ap[0:4, 512:1024], ALU.subtract, fr1.rg() + pk.rg(512, 1024), pk.rg(1024, 1536))
            for hh in range(4):
                for pc in range(3):
                    c0 = hh * 4096 + t * 512
                    dma("pool", KT.ap[64 + pc:65 + pc, c0:c0 + 512], pk.ap[hh:hh + 1, pc * 512:(pc + 1) * 512],
                        pk.rg(pc * 512, (pc + 1) * 512), KT.rg(c0, c0 + 512) + [("augk_t", t)], ("augk", t), join=True)
            ca, cb = co[0], co[1]
            ts("dve", ca.ap[0:4, :], cc.ap[0:4, 0:128], selw_sb.ap[0:4, 0:1], None, ALU.mult, None, cc.rg() + selw_sb.rg(), ca.rg())
            src_, dst_ = ca, cb
            for s_ in range(1, 4):
                stt("dve", dst_.ap[0:4, :], cc.ap[0:4, s_ * 128:(s_ + 1) * 128], selw_sb.ap[0:4, s_:s_ + 1], src_.ap[0:4, :],
                    ALU.mult, ALU.add, cc.rg() + selw_sb.rg() + src_.rg(), dst_.rg())
                src_, dst_ = dst_, src_
            cown = src_
            ts("dve", pq.ap[0:4, 0:128], cown.ap[0:4, :], -8.0, None, ALU.mult, None, cown.rg(), pq.rg(0, 128))
            stt("dve", qr1.ap[0:4, :], cown.ap[0:4, :], -8.0, pq.ap[0:4, 0:128], ALU.mult, ALU.subtract, cown.rg() + pq.rg(0, 128), qr1.rg())
            cp("dve", pq.ap[0:4, 128:256], qr1.ap[0:4, :], qr1.rg(), pq.rg(128, 256))
            tt("dve", pq.ap[0:4, 256:384], qr1.ap[0:4, :], pq.ap[0:4, 128:256], ALU.subtract, qr1.rg() + pq.rg(128, 256), pq.rg(256, 384))
            for hh in range(4):
                for pc in range(3):
                    c0 = hh * 1024 + t * 128
                    dma("pool", QT.ap[67 + pc:68 + pc, c0:c0 + 128], pq.ap[hh:hh + 1, pc * 128:(pc + 1) * 128],
                        pq.rg(pc * 128, (pc + 1) * 128), QT.rg(c0, c0 + 128) + [("augq_t", t)], ("augq", t), join=True)

        if p == 0 and early(2):
            return nc

        def k_ops(hh, i, s):
            kc0 = hh * 4096 + s * 128
            qc0 = hh * 1024 + i * 128
            return ([(KT.ap[0:70, kc0:kc0 + 128], QT.ap[0:70, qc0:qc0 + 128])],
                    KT.rg(kc0, kc0 + 128) + QT.rg(qc0, qc0 + 128) + [("augk_t", s // 4), ("augq_t", i)])

        def v_ap(hh, s):
            c0 = s * 260 + hh * 65
            return VA.ap[:, c0:c0 + 65]

        def v_rg(hh, s):
            c0 = s * 260 + hh * 65
            return VA.rg(c0, c0 + 65)

        nxt_own = []

        def own_hook(i, nxt_own=nxt_own, p=p):
            if p < 3 and i in (5, 6):
                nxt_own.append(stream_uT(stq, xT_own, i - 5))
        attention(4, 64, 0.125, mF, k_ops, v_ap, v_rg, p * 256, after_slot=own_hook)
        pre_own = nxt_own if p < 3 else None
        if p == 0 and early(3):
            return nc

    if early(4):
        return nc
    top[0] = mark_att
    CK = alloc(2 * 4096, BF16)
    KR = alloc(4096, BF16)
    CQ = alloc(4 * 1024, BF16)
    RQ = alloc(1024, F32)
    RTQ = alloc(1024, F32)
    mark_mla = top[0]
    stq = make_stream(nx=6)
    sq = alloc(4 * 512, BF16)
    ckraw = alloc(2 * 512, BF16)
    rk = alloc(512, F32)
    rt = [alloc(512, F32) for _ in range(2)]
    t1 = alloc(512, F32)
    t2 = alloc(512, F32)
    dma("sp", RTQ.ap, ropeQ, [], RTQ.rg(), "rtq")
    memset("pool", KR.ap[64:128, :], 0.0, KR.rg())

    Wcq0, Wcq0_r = wload(wcq[:, 0:256], KC, 256)
    Wcq1, Wcq1_r = wload(wcq[:, 256:512], KC, 256)
    for n in range(2):
        ub, base, stride = stream_uT(stq, xT_own, n)
        for m in range(4):
            Wc, Wc_r = (Wcq0, Wcq0_r) if m < 2 else (Wcq1, Wcq1_r)
            b = nextb()
            mm(PS[b][:, :], [(Wc[:, k, (m % 2) * 128:(m % 2 + 1) * 128], uT_ap(ub, base, stride, k)) for k in range(KC)],
               Wc_r + ub.rg(), psr(b))
            c0 = m * 1024 + n * 512
            cp("act", CQ.ap[:, c0:c0 + 512], PS[b][:, :], psr(b), CQ.rg(c0, c0 + 512))
            act(sq.ap[:, m * 512:(m + 1) * 512], PS[b][:, :], AF.Square, psr(b), sq.rg(m * 512, (m + 1) * 512))
        b = nextb()
        mm(PS[b][:, :], [(ones_bf.ap[:, :], sq.ap[:, m * 512:(m + 1) * 512]) for m in range(4)], ones_bf.rg() + sq.rg(), psr(b))
        ts("dve", rk.ap, PS[b][:, :], 1.0 / 512, RMS_EPS, ALU.mult, ALU.add, psr(b), rk.rg())
        act(rk.ap, rk.ap, AF.Ln, rk.rg(), rk.rg())
        act(RQ.ap[:, n * 512:(n + 1) * 512], rk.ap, AF.Exp, rk.rg(), RQ.rg(n * 512, (n + 1) * 512), scale=-0.5)

    Wckv, Wckv_r = wload(wckr[:, 0:256], KC, 256)
    Wkr, Wkr_r = wload(wckr[:, 256:384], KC, 128)
    nxt = stream_uT(stq, xT_seq, 0)
    for t in range(8):
        ub, base, stride = nxt
        if t + 1 < 8:
            nxt = stream_uT(stq, xT_seq, t + 1)
        rtb = rt[t % 2]
        dma("sp", rtb.ap, ropeK[:, t * 512:(t + 1) * 512], [], rtb.rg(), ("rt", t % 2))
        for m in range(2):
            b = nextb()
            mm(PS[b][:, :], [(Wckv[:, k, m * 128:(m + 1) * 128], uT_ap(ub, base, stride, k)) for k in range(KC)],
               Wckv_r + ub.rg(), psr(b))
            cp("act", ckraw.ap[:, m * 512:(m + 1) * 512], PS[b][:, :], psr(b), ckraw.rg(m * 512, (m + 1) * 512))
            act(sq.ap[:, m * 512:(m + 1) * 512], PS[b][:, :], AF.Square, psr(b), sq.rg(m * 512, (m + 1) * 512))
        b = nextb()
        mm(PS[b][:, :], [(ones_bf.ap[:, :], sq.ap[:, m * 512:(m + 1) * 512]) for m in range(2)], ones_bf.rg() + sq.rg(0, 1024), psr(b))
        ts("dve", t1.ap, PS[b][:, :], 1.0 / 256, RMS_EPS, ALU.mult, ALU.add, psr(b), t1.rg())
        act(t1.ap, t1.ap, AF.Ln, t1.rg(), t1.rg())
        act(rk.ap, t1.ap, AF.Exp, t1.rg(), rk.rg(), scale=-0.5)
        for m in range(2):
            c0 = m * 4096 + t * 512
            tt("dve", CK.ap[:, c0:c0 + 512], ckraw.ap[:, m * 512:(m + 1) * 512], rk.ap, ALU.mult,
               ckraw.rg(m * 512, (m + 1) * 512) + rk.rg(), CK.rg(c0, c0 + 512))
        b = nextb()
        mm(PS[b][:, :], [(Wkr[:, k, :], uT_ap(ub, base, stride, k)) for k in range(KC)], Wkr_r + ub.rg(), psr(b))
        tt("dve", t1.ap[0:64, :], PS[b][0:64, :], rtb.ap[0:64, :], ALU.mult, psr(b) + rtb.rg(), t1.rg())
        tt("dve", t2.ap[0:64, :], PS[b][64:128, :], rtb.ap[64:128, :], ALU.mult, psr(b) + rtb.rg(), t2.rg())
        tt("dve", KR.ap[0:64, t * 512:(t + 1) * 512], t1.ap[0:64, :], t2.ap[0:64, :], ALU.add, t1.rg() + t2.rg(), KR.rg(t * 512, (t + 1) * 512))

    top[0] = mark_mla
    KN = alloc(2 * 4096, BF16)
    VM = alloc(32 * 2 * 129, BF16)
    QN = alloc(2 * 1024, BF16)
    QR = alloc(2 * 1024, BF16)
    WSETS = [(alloc(512, BF16), alloc(512, BF16), alloc(1024, BF16), alloc(1024, BF16)) for _ in range(2)]
    t1 = alloc(512, F32)
    t2 = alloc(512, F32)
    t3 = alloc(512, F32)
    abrow = [alloc(256, F32) for _ in range(2)]
    amrow = [alloc(256, F32) for _ in range(2)]
    aring = [alloc(4096, BF16) for _ in range(2)]
    actr = [16]

    def aload(src):
        wb = aring[actr[0] % 2]
        view = wb.ap[:, 0:4096].rearrange("p (k n) -> p k n", k=KC)
        dma("pool", wb.ap[:, 0:4096], src, [], wb.rg(), ("aw", actr[0] % 2))
        return view, wb.rg()

    def ada_tail(i):
        if actr[0] < 48:
            nextps_fn[0] = nextb
            ada_slab(actr[0], abrow, amrow, aload)
            nextps_fn[0] = nextps
            actr[0] += 1

    memset("pool", QR.ap[64:128, :], 0.0, QR.rg())
    memset("pool", VM.ap[:, :], 1.0, VM.rg())
    bb_n[0] = 2
    def prep_weights(sp_):
        WKN, WVM, WQN, WQR = WSETS[sp_ % 2]
        c0w = sp_ * 256
        wa, wa_r = wload(wkn[:, c0w:c0w + 256], 2, 256)
        wb_, wb_r = wload(wvm[:, c0w:c0w + 256], 2, 256)
        for m in range(2):
            ts("dve", WKN.ap[:, m * 256:(m + 1) * 256], wa[:, m, :], gkv_sb.ap[:, m:m + 1], None, ALU.mult, None,
               wa_r + gkv_sb.rg(), WKN.rg(m * 256, (m + 1) * 256))
            ts("dve", WVM.ap[:, m * 256:(m + 1) * 256], wb_[:, m, :], gkv_sb.ap[:, m:m + 1], None, ALU.mult, None,
               wb_r + gkv_sb.rg(), WVM.rg(m * 256, (m + 1) * 256))
        wc, wc_r = wload(wqn[:, c0w:c0w + 256], 4, 256)
        wd, wd_r = wload(wqr[:, c0w:c0w + 256], 4, 256)
        for m in range(4):
            ts("dve", WQN.ap[:, m * 256:(m + 1) * 256], wc[:, m, :], gq_sb.ap[:, m:m + 1], None, ALU.mult, None,
               wc_r + gq_sb.rg(), WQN.rg(m * 256, (m + 1) * 256))
            ts("dve", WQR.ap[:, m * 256:(m + 1) * 256], wd[:, m, :], gq_sb.ap[:, m:m + 1], None, ALU.mult, None,
               wd_r + gq_sb.rg(), WQR.rg(m * 256, (m + 1) * 256))

    prep_weights(0)
    for sp_ in range(4):
        WKN, WVM, WQN, WQR = WSETS[sp_ % 2]

        def build_units(i, WKN=WKN, WVM=WVM, WQN=WQN, WQR=WQR):
            units = []
            tq = i

            def kn_unit(hh):
                b = nextb()
                mm(PS[b][:, :], [(WKN.ap[:, m * 256 + hh * 128:m * 256 + (hh + 1) * 128], CK.ap[:, m * 4096 + tq * 512:m * 4096 + (tq + 1) * 512])
                                 for m in range(2)],
                   WKN.rg() + CK.rg(tq * 512, (tq + 1) * 512) + CK.rg(4096 + tq * 512, 4096 + (tq + 1) * 512), psr(b))
                c0 = hh * 4096 + tq * 512
                cp("dve", KN.ap[:, c0:c0 + 512], PS[b][:, :], psr(b), KN.rg(c0, c0 + 512))

            def v_unit(tile):
                b = nextb()
                mm(PS[b][:, 0:256], [(CK.ap[:, m * 4096 + tile * 128:m * 4096 + (tile + 1) * 128], WVM.ap[:, m * 256:(m + 1) * 256]) for m in range(2)],
                   WVM.rg() + CK.rg(tile * 128, (tile + 1) * 128) + CK.rg(4096 + tile * 128, 4096 + (tile + 1) * 128), psr(b))
                c0 = tile * 258
                ov = VM.ap[:, c0:c0 + 258].rearrange("p (h d) -> p h d", d=129)[:, :, 0:128]
                iv = PS[b][:, 0:256].rearrange("p (h d) -> p h d", d=128)
                cp("dve", ov, iv, psr(b), VM.rg(c0, c0 + 258))

            def q_unit(n, hh):
                b = nextb()
                mm(PS[b][:, :], [(WQN.ap[:, m * 256 + hh * 128:m * 256 + (hh + 1) * 128], CQ.ap[:, m * 1024 + n * 512:m * 1024 + (n + 1) * 512])
                                 for m in range(4)], WQN.rg() + CQ.rg(), psr(b))
                c0 = hh * 1024 + n * 512
                tt("dve", QN.ap[:, c0:c0 + 512], PS[b][:, :], RQ.ap[:, n * 512:(n + 1) * 512], ALU.mult,
                   psr(b) + RQ.rg(n * 512, (n + 1) * 512), QN.rg(c0, c0 + 512))
                b = nextb()
                mm(PS[b][:, :], [(WQR.ap[:, m * 256 + hh * 128:m * 256 + (hh + 1) * 128], CQ.ap[:, m * 1024 + n * 512:m * 1024 + (n + 1) * 512])
                                 for m in range(4)], WQR.rg() + CQ.rg(), psr(b))
                tt("dve", t1.ap[0:64, :], PS[b][0:64, :], RTQ.ap[0:64, n * 512:(n + 1) * 512], ALU.mult, psr(b) + RTQ.rg(), t1.rg())
                tt("dve", t2.ap[0:64, :], PS[b][64:128, :], RTQ.ap[64:128, n * 512:(n + 1) * 512], ALU.mult, psr(b) + RTQ.rg(), t2.rg())
                tt("dve", t3.ap[0:64, :], t1.ap[0:64, :], t2.ap[0:64, :], ALU.add, t1.rg() + t2.rg(), t3.rg())
                tt("dve", QR.ap[0:64, c0:c0 + 512], t3.ap[0:64, :], RQ.ap[0:64, n * 512:(n + 1) * 512], ALU.mult,
                   t3.rg() + RQ.rg(n * 512, (n + 1) * 512), QR.rg(c0, c0 + 512))

            if i % 4 == 0:
                for hh in range(2):
                    units.append(lambda hh=hh: q_unit(i // 4, hh))
            for hh in range(2):
                units.append(lambda hh=hh: kn_unit(hh))
            for tile in range(4 * i, 4 * i + 4):
                units.append(lambda tile=tile: v_unit(tile))
            return units

        def k_ops(hh, i, s):
            kc0 = hh * 4096 + s * 128
            qc0 = hh * 1024 + i * 128
            return ([(KN.ap[:, kc0:kc0 + 128], QN.ap[:, qc0:qc0 + 128]),
                     (KR.ap[:, s * 128:(s + 1) * 128], QR.ap[:, qc0:qc0 + 128])],
                    KN.rg(kc0, kc0 + 128) + QN.rg(qc0, qc0 + 128) + KR.rg(s * 128, (s + 1) * 128) + QR.rg(qc0, qc0 + 128))

        def v_ap(hh, s):
            c0 = s * 258 + hh * 129
            return VM.ap[:, c0:c0 + 129]

        def v_rg(hh, s):
            c0 = s * 258 + hh * 129
            return VM.rg(c0, c0 + 129)

        def mla_hook(i, sp_=sp_):
            ada_tail(i)
            if i == 3 and sp_ < 3:
                prep_weights(sp_ + 1)
        attention(2, 128, 192.0 ** -0.5, mM, k_ops, v_ap, v_rg, 1024 + sp_ * 256, after_slot=mla_hook, build_units=build_units)

    if early(5):
        return nc
    dbg_keys = []
    if debug:
        for nm, bf_, dt_ in (("KN", KN, BF16), ("KR", KR, BF16), ("QN", QN, BF16), ("QR", QR, BF16), ("VM", VM, BF16),
                            ("CK", CK, BF16), ("CQ", CQ, BF16), ("RQ", RQ, F32)):
            dd = dram("dbg_" + nm, [128, bf_.cols], kind="ExternalOutput", dt=dt_)
            dma("sp", dd, bf_.ap, bf_.rg(), [], "dbgx" + nm)
            dbg_keys.append("dbgx" + nm)
        d_yb = dram("dbg_yb", [128, 16384], kind="ExternalOutput", dt=BF16)
        dma("sp", d_yb, yb.ap, yb.rg(), [], "dbg0")
        dbg_keys.append("dbg0")
    top[0] = mark_att
    yT = alloc(16 * 1024, BF16)
    UO = alloc(16 * 1024, BF16)
    MT = alloc(16 * 1024, BF16)
    mark_mg = top[0]
    stq = make_stream(nx=3, nu=0)
    sA = alloc(512, F32)
    sB = alloc(512, F32)
    m1 = alloc(512, F32)
    m2 = alloc(512, F32)
    for i in range(8):
        for cq in range(4):
            b = nextps()
            pv = PS[b]

            def fn(e, pv=pv, i=i, cq=cq):
                ins = None
                for c4 in range(4):
                    c = cq * 4 + c4
                    ins = e.matmul(pv[:, c4 * 128:(c4 + 1) * 128], lhsT=yb.ap[:, i * 2048 + c * 128:i * 2048 + (c + 1) * 128],
                                   rhs=ident.ap[:, :], start=(c4 == 0), stop=(c4 == 3), skip_group_check=True)
                return ins
            P.op("pe", fn, R=yb.rg(i * 2048 + cq * 512, i * 2048 + (cq + 1) * 512) + ident.rg(), W=psr(b))
            for c4 in range(4):
                c = cq * 4 + c4
                c0 = c * 1024 + i * 128
                cp("act" if cq % 2 == 0 else "dve", yT.ap[:, c0:c0 + 128], pv[:, c4 * 128:(c4 + 1) * 128], psr(b), yT.rg(c0, c0 + 128))
    if early(51):
        return nc
    for n in range(2):
        stream_uT(stq, xT_own, n, dst=UO, dcol=n * 512)
    if early(52):
        return nc
    extra = [Buf(arena_ap, yb.off + q * 8192, 4096, BF16) for q in range(4)]
    wring.extend(extra)
    for m in range(16):
        if m == 1 and early(53):
            return nc
        s1 = wslot()
        Wgf_, Wgf_r = wload(wgf[:, m * 128:(m + 1) * 128], KC, 128, slot=s1, col0=0)
        Wgm_, Wgm_r = wload(wgm[:, m * 128:(m + 1) * 128], KC, 128, slot=s1, col0=2048)
        s2 = wslot()
        Wbf_, Wbf_r = wload(wbf[:, m * 128:(m + 1) * 128], 8, 128, slot=s2, col0=0)
        Wbm_, Wbm_r = wload(wbm[:, m * 128:(m + 1) * 128], 8, 128, slot=s2, col0=1024)
        for n in range(2):
            bA, bB, bPA, bPB = nextps(), nextps(), nextps(), nextps()
            mm(PS[bPA][:, :], [(Wbf_[:, k, :], yT.ap[:, k * 1024 + n * 512:k * 1024 + (n + 1) * 512]) for k in range(8)], wring[s2].rg() + yT.rg(0, 8192), psr(bPA))
            mm(PS[bPB][:, :], [(Wbm_[:, k, :], yT.ap[:, (8 + k) * 1024 + n * 512:(8 + k) * 1024 + (n + 1) * 512]) for k in range(8)],
               wring[s2].rg() + yT.rg(8192, 16384), psr(bPB))
            mm(PS[bA][:, :], [(Wgf_[:, k, :], UO.ap[:, k * 1024 + n * 512:k * 1024 + (n + 1) * 512]) for k in range(KC)], wring[s1].rg() + UO.rg(), psr(bA))
            mm(PS[bB][:, :], [(Wgm_[:, k, :], UO.ap[:, k * 1024 + n * 512:k * 1024 + (n + 1) * 512]) for k in range(KC)], wring[s1].rg() + UO.rg(), psr(bB))
            act(sA.ap, PS[bA][:, :], AF.Sigmoid, psr(bA), sA.rg())
            act(sB.ap, PS[bB][:, :], AF.Sigmoid, psr(bB), sB.rg())
            tt("dve", m1.ap, PS[bPA][:, :], sA.ap, ALU.mult, psr(bPA) + sA.rg(), m1.rg())
            tt("dve", m2.ap, PS[bPB][:, :], sB.ap, ALU.mult, psr(bPB) + sB.rg(), m2.rg())
            c0 = m * 1024 + n * 512
            tt("dve", MT.ap[:, c0:c0 + 512], m1.ap, m2.ap, ALU.add, m1.rg() + m2.rg(), MT.rg(c0, c0 + 512))

    if early(6):
        return nc
    if debug:
        d_mt = dram("dbg_mt", [128, 16384], kind="ExternalOutput", dt=BF16)
        dma("sp", d_mt, MT.ap, MT.rg(), [], "dbg1")
        dbg_keys.append("dbg1")
    ACC = Buf(arena_ap, yb.off, 8 * 2048, F32)
    assert yb.off + 8 * 2048 * 4 <= UO.off
    U2T = UO
    _MT = MT
    for q in range(4):
        wring.pop()
    extra = [Buf(arena_ap, UO.off + q * 8192, 4096, BF16) for q in range(4)]
    wring.extend(extra)
    top[0] = mark_mg
    G1B = alloc(2048, BF16)
    bcast_rows(G1B, G1)
    xo = [alloc(512, F32) for _ in range(4)]
    xoc = [0]
    for j in range(4):
        sa_ = wslot()
        Wa, Wa_r = wload(wout[0:1024, j * 512:(j + 1) * 512], 8, 512, slot=sa_)
        sb2 = wslot()
        Wb, Wb_r = wload(wout[1024:2048, j * 512:(j + 1) * 512], 8, 512, slot=sb2)
        for (Wx, Wx_r, sl) in ((Wa, Wa_r, sa_), (Wb, Wb_r, sb2)):
            for k in range(8):
                tt("pool", Wx[:, k, :], Wx[:, k, :], G1B.ap[:, j * 512:(j + 1) * 512], ALU.mult,
                   wring[sl].rg(k * 512, (k + 1) * 512) + G1B.rg(j * 512, (j + 1) * 512), wring[sl].rg(k * 512, (k + 1) * 512))
        for tq in range(8):
            xb = xo[xoc[0] % 4]
            xoc[0] += 1
            dma("sp", xb.ap, x_own[tq * 128:(tq + 1) * 128, j * 512:(j + 1) * 512], [], xb.rg(), ("xo", id(xb)))
            b = nextps()
            mm(PS[b][:, :], [((_MT.ap[:, k * 1024 + tq * 128:k * 1024 + (tq + 1) * 128]), (Wa if k < 8 else Wb)[:, k % 8, :]) for k in range(KC)],
               _MT.rg() + wring[sa_].rg() + wring[sb2].rg(), psr(b))
            c0 = tq * 2048 + j * 512
            stt("dve", ACC.ap[:, c0:c0 + 512], xb.ap, ALPHA, PS[b][:, :], ALU.mult, ALU.add, xb.rg() + psr(b), ACC.rg(c0, c0 + 512))

    if debug:
        d_a1 = dram("dbg_acc1", [128, 16384], kind="ExternalOutput")
        dma("sp", d_a1, ACC.ap, ACC.rg(), [], "dbg2")
        dbg_keys.append("dbg2")
    for q in range(4):
        wring.pop()
    top[0] = mark_mg
    print('mark_mg', mark_mg) if DEBUG_SRC else None
    stats = alloc(4 * 6, F32)
    mv = alloc(2, F32)
    rstd = alloc(1, F32)
    tmpv = alloc(1, F32)

    def layernorm_stats(src_ap_fn, src_rg, stats, mv, rstd, tmpv):
        for c in range(4):
            P.op("dve", lambda e, c=c, o_=stats.ap[:, c * 6:(c + 1) * 6], i_=src_ap_fn(c): e.bn_stats(out=o_, in_=i_),
                 R=src_rg, W=stats.rg(c * 6, (c + 1) * 6))
        P.op("dve", lambda e, o_=mv.ap[:, 0:2], i_=stats.ap[:, 0:24]: e.bn_aggr(out=o_, in_=i_), R=stats.rg(), W=mv.rg())
        ts("dve", tmpv.ap, mv.ap[:, 1:2], LN_EPS, None, ALU.add, None, mv.rg(), tmpv.rg())
        act(tmpv.ap, tmpv.ap, AF.Ln, tmpv.rg(), tmpv.rg())
        act(rstd.ap, tmpv.ap, AF.Exp, tmpv.rg(), rstd.rg(), scale=-0.5)

    LNG = Buf(arena_ap, MT.off, 2048, F32)
    LNB = Buf(arena_ap, MT.off + 8192, 2048, F32)
    dma("sp", LNG.ap, lnp[0:1, :].broadcast_to([128, D]), [], LNG.rg(), "lng")
    dma("sp", LNB.ap, lnp[1:2, :].broadcast_to([128, D]), [], LNB.rg(), "lnb")
    tt("dve", g2c.ap, lnc_sb.ap[:, 0:16], modT.ap[:, SC2:SC2 + 16], ALU.mult, lnc_sb.rg() + modT.rg(SC2, SC2 + 16), g2c.rg())
    tt("dve", b2c.ap, lnc_sb.ap[:, 16:32], modT.ap[:, SC2:SC2 + 16], ALU.mult, lnc_sb.rg() + modT.rg(SC2, SC2 + 16), b2c.rg())
    tt("dve", b2c.ap, b2c.ap, modT.ap[:, SH2:SH2 + 16], ALU.add, b2c.rg() + modT.rg(SH2, SH2 + 16), b2c.rg())
    xns = [alloc(2048, F32) for _ in range(2)]
    xnbs = [alloc(2048, BF16) for _ in range(2)]
    sts = [(stats, mv, rstd, tmpv), (alloc(4 * 6, F32), alloc(2, F32), alloc(1, F32), alloc(1, F32))]

    def ln1_main(tq):
        a0 = tq * 2048
        xn, xnb = xns[tq % 2], xnbs[tq % 2]
        st_, mv_, rs_, tv_ = sts[tq % 2]
        layernorm_stats(lambda c, a0=a0: ACC.ap[:, a0 + c * 512:a0 + (c + 1) * 512], ACC.rg(a0, a0 + 2048), st_, mv_, rs_, tv_)
        ts("dve", xn.ap, ACC.ap[:, a0:a0 + 2048], mv_.ap[:, 0:1], rs_.ap[:, 0:1], ALU.subtract, ALU.mult,
           ACC.rg(a0, a0 + 2048) + mv_.rg() + rs_.rg(), xn.rg())
        cp("act", xnb.ap, xn.ap, xn.rg(), xnb.rg())
        for cq in range(4):
            b = nextps()
            pv = PS[b]

            def fn(e, pv=pv, cq=cq, xnb=xnb):
                ins = None
                for c4 in range(4):
                    c = cq * 4 + c4
                    ins = e.matmul(pv[:, c4 * 128:(c4 + 1) * 128], lhsT=xnb.ap[:, c * 128:(c + 1) * 128],
                                   rhs=ident.ap[:, :], start=(c4 == 0), stop=(c4 == 3), skip_group_check=True)
                return ins
            P.op("pe", fn, R=xnb.rg(cq * 512, (cq + 1) * 512) + ident.rg(), W=psr(b))
            for c4 in range(4):
                c = cq * 4 + c4
                c0 = c * 1024 + tq * 128
                if cq % 2 == 0:
                    act(U2T.ap[:, c0:c0 + 128], pv[:, c4 * 128:(c4 + 1) * 128], AF.Identity, psr(b) + g2c.rg() + b2c.rg(), U2T.rg(c0, c0 + 128),
                        bias=b2c.ap[:, c:c + 1], scale=g2c.ap[:, c:c + 1])
                else:
                    ts("dve", U2T.ap[:, c0:c0 + 128], pv[:, c4 * 128:(c4 + 1) * 128], g2c.ap[:, c:c + 1], b2c.ap[:, c:c + 1], ALU.mult, ALU.add,
                       psr(b) + g2c.rg() + b2c.rg(), U2T.rg(c0, c0 + 128))

    def ln1_side(tq):
        a0 = tq * 2048
        xn = xns[tq % 2]
        tt("pool", xn.ap, xn.ap, LNG.ap, ALU.mult, xn.rg() + LNG.rg(), xn.rg())
        tt("pool", xn.ap, xn.ap, LNB.ap, ALU.add, xn.rg() + LNB.rg(), xn.rg())
        act(ACC.ap[:, a0:a0 + 2048], xn.ap, AF.Copy, xn.rg(), ACC.rg(a0, a0 + 2048), scale=ALPHA)

    for tq in range(8):
        ln1_main(tq)
        if tq >= 1:
            ln1_side(tq - 1)
    ln1_side(7)

    if early(7):
        return nc
    top[0] = MT.off
    G2B = alloc(2048, BF16)
    bcast_rows(G2B, G2)
    HR = [alloc(512, BF16) for _ in range(2)]
    HT = [alloc(4 * 1024, BF16) for _ in range(2)]
    xr = [alloc(4096, BF16) for _ in range(3)]
    wring.extend(xr)
    hrc = [0]
    for f in range(16):
        su0, su1 = wslot(), wslot()
        Wu0, Wu0_r = wload(wup[0:1024, f * 512:(f + 1) * 512], 8, 512, slot=su0)
        Wu1, Wu1_r = wload(wup[1024:2048, f * 512:(f + 1) * 512], 8, 512, slot=su1)
        sd0, sd1 = wslot(), wslot()
        Wd0, Wd0_r = wload(wdn[f * 512:f * 512 + 256, :], 2, 2048, slot=sd0)
        Wd1, Wd1_r = wload(wdn[f * 512 + 256:(f + 1) * 512, :], 2, 2048, slot=sd1)
        for (Wx, sl) in ((Wd0, sd0), (Wd1, sd1)):
            for k in range(2):
                tt("pool", Wx[:, k, :], Wx[:, k, :], G2B.ap, ALU.mult,
                   wring[sl].rg(k * 2048, (k + 1) * 2048) + G2B.rg(), wring[sl].rg(k * 2048, (k + 1) * 2048))
        ht = HT[f % 2]
        for hc in range(4):
            for n in range(2):
                b = nextps()
                mm(PS[b][:, :], [((Wu0 if k < 8 else Wu1)[:, k % 8, hc * 128:(hc + 1) * 128], U2T.ap[:, k * 1024 + n * 512:k * 1024 + (n + 1) * 512])
                                 for k in range(KC)], wring[su0].rg() + wring[su1].rg() + U2T.rg(), psr(b))
                hr = HR[hrc[0] % 2]
                hrc[0] += 1
                act(hr.ap, PS[b][:, :], AF.Relu, psr(b), hr.rg())
                c0 = hc * 1024 + n * 512
                tt("pool", ht.ap[:, c0:c0 + 512], hr.ap, hr.ap, ALU.mult, hr.rg(), ht.rg(c0, c0 + 512))
        for tq in range(8):
            for j in range(4):
                b = nextps()
                mm(PS[b][:, :], [(ht.ap[:, hc * 1024 + tq * 128:hc * 1024 + (tq + 1) * 128], (Wd0 if hc < 2 else Wd1)[:, hc % 2, j * 512:(j + 1) * 512])
                                 for hc in range(4)], ht.rg() + wring[sd0].rg() + wring[sd1].rg(), psr(b))
                c0 = tq * 2048 + j * 512
                tt("dve", ACC.ap[:, c0:c0 + 512], ACC.ap[:, c0:c0 + 512], PS[b][:, :], ALU.add, ACC.rg(c0, c0 + 512) + psr(b), ACC.rg(c0, c0 + 512))

    if early(8):
        return nc
    if debug:
        d_a2 = dram("dbg_acc2", [128, 16384], kind="ExternalOutput")
        dma("sp", d_a2, ACC.ap, ACC.rg(), [], "dbg3")
        dbg_keys.append("dbg3")
    top[0] = MT.off
    LNG2 = alloc(2048, F32)
    LNB2 = alloc(2048, F32)
    dma("sp", LNG2.ap, lnp[2:3, :].broadcast_to([128, D]), [], LNG2.rg(), "lng2")
    dma("sp", LNB2.ap, lnp[3:4, :].broadcast_to([128, D]), [], LNB2.rg(), "lnb2")
    stats = alloc(4 * 6, F32)
    mv = alloc(2, F32)
    rstd = alloc(1, F32)
    tmpv = alloc(1, F32)
    ob_ = [alloc(2048, F32) for _ in range(2)]
    for tq in range(8):
        a0 = tq * 2048
        o_ = ob_[tq % 2]
        layernorm_stats(lambda c, a0=a0: ACC.ap[:, a0 + c * 512:a0 + (c + 1) * 512], ACC.rg(a0, a0 + 2048), stats, mv, rstd, tmpv)
        ts("dve", o_.ap, ACC.ap[:, a0:a0 + 2048], mv.ap[:, 0:1], rstd.ap[:, 0:1], ALU.subtract, ALU.mult,
           ACC.rg(a0, a0 + 2048) + mv.rg() + rstd.rg(), o_.rg())
        tt("pool", o_.ap, o_.ap, LNG2.ap, ALU.mult, o_.rg() + LNG2.rg(), o_.rg())
        tt("dve", o_.ap, o_.ap, LNB2.ap, ALU.add, o_.rg() + LNB2.rg(), o_.rg())
        dma("sp", out_d[tq * 128:(tq + 1) * 128, :], o_.ap, o_.rg(), [], "out")

    P.emit(st, ["out"] + dbg_keys)
    st.close()
    return nc


def _prep(inputs):
    f = lambda a: np.ascontiguousarray(np.asarray(a, dtype=np.float32))
    x = f(inputs["x"])
    c = f(inputs["c"])
    w_in = f(inputs["w_in"])[0]
    cuts = np.cumsum([3072, 16, 512, 256, 64, 2048, 2048])
    qkv = w_in[:, :3072]
    wf_ = w_in[:, 3072:3088]
    wcq_ = w_in[:, 3088:3600]
    wckv_ = w_in[:, 3600:3856]
    wkr_ = w_in[:, 3856:3920]
    wgf_ = w_in[:, 3920:5968]
    wgm_ = w_in[:, 5968:8016]
    wkr_sw = np.concatenate([wkr_[:, 32:], wkr_[:, :32]], axis=1)
    wqup = f(inputs["w_q_up"])[0].reshape(512, 8, 192)
    wqn_ = np.ascontiguousarray(wqup[:, :, :128].reshape(512, 1024))
    qr = wqup[:, :, 128:]
    wqr_ = np.ascontiguousarray(np.concatenate([qr, qr[:, :, 32:], qr[:, :, :32]], axis=2).reshape(512, 1024))
    wkvup = f(inputs["w_kv_up"])[0].reshape(256, 8, 256)
    wkn_ = np.ascontiguousarray(wkvup[:, :, :128].reshape(256, 1024))
    wvm_ = np.ascontiguousarray(wkvup[:, :, 128:].reshape(256, 1024))
    bfv = f(inputs["b_forget"])[0]
    shared = {
        "w_ada": f(f(inputs["w_ada"])[0].reshape(16, 128, 48, 256).transpose(2, 1, 0, 3).reshape(48 * 128, 4096)),
        "b_ada": f(inputs["b_ada"])[0][None, :],
        "wq": f(qkv[:, 0:1024]), "wk": f(qkv[:, 1024:2048]), "wv": f(qkv[:, 2048:3072]),
        "wf_r": f(wf_.reshape(16, 128, 16).transpose(1, 0, 2).reshape(128, 256)),
        "wcq": f(wcq_), "wckr": f(np.concatenate([wckv_, wkr_, wkr_sw], axis=1)),
        "wgf": f(wgf_), "wgm": f(wgm_),
        "bfg": f(bfv.reshape(4, 4).T),
        "gq": f(f(inputs["g_q_norm"])[0].reshape(4, 128).T),
        "gkv": f(f(inputs["g_kv_norm"])[0].reshape(2, 128).T),
        "wqn": wqn_, "wqr": wqr_, "wkn": wkn_, "wvm": wvm_,
        "wbf": f(inputs["w_branch_fox"])[0], "wbm": f(inputs["w_branch_mla"])[0],
        "wout": f(inputs["w_out"])[0],
        "lnp": f(np.stack([f(inputs["ln1_g"])[0], f(inputs["ln1_b"])[0], f(inputs["ln2_g"])[0], f(inputs["ln2_b"])[0]])),
        "wup": f(inputs["w_mlp_up"])[0], "wdn": f(inputs["w_mlp_down"])[0],
        "lnc": f(np.concatenate([f(inputs[k_])[0].reshape(16, 128).T for k_ in ("ln1_g", "ln1_b", "ln2_g", "ln2_b")], axis=1)),
        "ident": np.eye(128, dtype=np.float32),
    }
    pos = np.arange(S, dtype=np.float32)
    inv = (10000.0 ** (-np.arange(0, 64, 2, dtype=np.float32) / 64)).astype(np.float32)
    ang = (pos[:, None] * inv[None, :]).astype(np.float32)
    cos, sin = np.cos(ang).T.astype(np.float32), np.sin(ang).T.astype(np.float32)
    ropeK = f(np.concatenate([cos, cos, -sin, sin], axis=0))
    kk = np.arange(128)[:, None]
    qq = np.arange(128)[None, :]
    tri = np.where(kk <= qq, 0.0, NEG).astype(np.float32)
    chk = np.where((kk // 64) <= (qq // 64), 0.0, NEG).astype(np.float32)
    in_maps = []
    for cidx in range(8):
        b, j = cidx // 4, cidx % 4
        blocks = [4 * i + j for i in range(8)]
        tok = np.concatenate([np.arange(g * 128, (g + 1) * 128) for g in blocks])
        xb = x[b]
        mF_ = np.zeros((128, 4, 4, 128), np.float32)
        mM_ = np.zeros((128, 4, 4, 128), np.float32)
        for s in range(4):
            if s == j:
                mF_[:, s] = tri[:, None, :]
                mM_[:, s] = chk[:, None, :]
            elif s > j:
                mF_[:, s] = NEG
                mM_[:, s] = NEG
        sel = np.zeros((128, 4), np.float32)
        sel[:, j] = 1.0
        m = dict(shared)
        m.update({
            "xT_seq": f(xb.T), "xT_own": f(xb[tok].T), "x_own": f(xb[tok]),
            "c_col": f(c[b].reshape(16, 128).T),
            "ropeK": ropeK, "ropeQ": f(ropeK[:, tok]),
            "maskF": f(mF_.reshape(128, 2048)), "maskM": f(mM_.reshape(128, 2048)),
            "selw": sel,
        })
        in_maps.append((m, b, tok))
    return in_maps


_NC_CACHE = {}


def kernel(**inputs):
    prepped = _prep(inputs)
    if "nc" not in _NC_CACHE:
        _NC_CACHE["nc"] = build_program()
    nc = _NC_CACHE["nc"]
    res = run_bass_kernel_spmd(nc, [m for (m, _, _) in prepped], core_ids=list(range(8)))
    out = np.zeros((2, S, D), np.float32)
    for (m, b, tok), r in zip(prepped, res.results):
        out[b, tok, :] = np.asarray(r["out"], dtype=np.float32)
    return out
```

```python
import numpy as np
from contextlib import ExitStack
import concourse.bass as bass
import concourse.mybir as mybir
from concourse.bass_utils import run_bass_kernel_spmd

F32 = mybir.dt.float32
BF16 = mybir.dt.bfloat16
ALU = mybir.AluOpType
AF = mybir.ActivationFunctionType

D = 2048
S = 4096
T = 1024
KC = 16
NEG = -30000.0
ALPHA = 2.0 ** 0.25
LN_EPS = 1e-5
RMS_EPS = 1e-6
ENGS = ("pe", "act", "dve", "pool", "sp")
GR = 256
DEBUG_SRC = False
SEG_PE_OPS = 10 ** 9


class Op:
    __slots__ = ("eng", "fn", "deps", "dkey", "dval", "sig", "sigval", "idx", "src")


class Prog:
    def __init__(self, nc):
        self.nc = nc
        self.ops = []
        self.w = {}
        self.r = {}
        self.dcount = {}

    def op(self, eng, fn, R=(), W=(), dkey=None, join=False):
        o = Op()
        o.eng, o.fn, o.dkey, o.idx = eng, fn, dkey, len(self.ops)
        o.sig, o.sigval, o.dval = False, 0, 0
        o.src = ""
        if DEBUG_SRC:
            import sys as _s
            fr = _s._getframe(1)
            ln = []
            while fr is not None and len(ln) < 4:
                ln.append(str(fr.f_lineno))
                fr = fr.f_back
            o.src = ",".join(ln)
        deps = set()
        ops = self.ops
        for k in R:
            ws = self.w.get(k)
            if ws:
                deps |= ws
        for k in W:
            ws = self.w.get(k)
            rs = self.r.get(k)
            jn = (join and dkey is not None and ws and not rs and all(ops[i].dkey == dkey for i in ws))
            if jn:
                ws.add(o.idx)
            else:
                if ws:
                    deps |= ws
                if rs:
                    deps |= rs
                self.w[k] = {o.idx}
                self.r[k] = set()
        for k in R:
            self.r.setdefault(k, set()).add(o.idx)
        deps.discard(o.idx)
        o.deps = deps
        if dkey is not None:
            self.dcount[dkey] = self.dcount.get(dkey, 0) + 16
            o.dval = self.dcount[dkey]
        ops.append(o)
        return o

    def emit(self, stack, final_wait_keys):
        nc = self.nc
        ops = self.ops
        for o in ops:
            for d in o.deps:
                do = ops[d]
                if do.dkey is None and not (do.eng == "pe" and o.eng == "pe"):
                    do.sig = True
        cnt = {e: 0 for e in ENGS}
        for o in ops:
            if o.dkey is None and o.sig:
                cnt[o.eng] += 1
                o.sigval = cnt[o.eng]
        esem = {e: stack.enter_context(nc.semaphore("s_" + e)) for e in ENGS}
        dsem = {k: stack.enter_context(nc.semaphore("d_%d" % i)) for i, k in enumerate(self.dcount)}
        import bisect
        klist = {}
        for o in ops:
            if o.dkey is not None:
                klist.setdefault(o.dkey, ([], []))
                klist[o.dkey][0].append(o.idx)
                klist[o.dkey][1].append(o.dval)
        bounds = [0]
        npe = 0
        for o in ops:
            if o.eng == "pe":
                npe += 1
            if npe >= SEG_PE_OPS:
                bounds.append(o.idx + 1)
                npe = 0
        if bounds[-1] != len(ops):
            bounds.append(len(ops))
        waited = {e: {} for e in ENGS}
        nseg = len(bounds) - 1
        for si in range(nseg):
            lo, hi = bounds[si], bounds[si + 1]
            per = {e: [o for o in ops[lo:hi] if o.eng == e] for e in ENGS}
            last = (si == nseg - 1)

            def run(engname, eng, per=per, last=last):
                wd = waited[engname]
                for o in per[engname]:
                    need = {}
                    for d in o.deps:
                        do = ops[d]
                        if do.dkey is not None:
                            idxs, vals = klist[do.dkey]
                            key, val = ("d", do.dkey), vals[bisect.bisect_left(idxs, o.idx) - 1]
                        else:
                            if do.eng == "pe" and engname == "pe":
                                continue
                            key, val = ("e", do.eng), do.sigval
                        if val > need.get(key, 0):
                            need[key] = val
                    for key, val in need.items():
                        if val > wd.get(key, 0):
                            wd[key] = val
                            sem = dsem[key[1]] if key[0] == "d" else esem[key[1]]
                            eng.wait_ge(sem, val)
                    ins = o.fn(eng)
                    if DEBUG_SRC:
                        ins.annotate("op%d@%s" % (o.idx, o.src))
                    if o.dkey is not None:
                        ins.then_inc(dsem[o.dkey], 16)
                    elif o.sig:
                        ins.then_inc(esem[engname], 1)
                if engname == "sp" and last:
                    for k in final_wait_keys:
                        eng.wait_ge(dsem[k], self.dcount[k])

            with nc.Block() as block:
                block.tensor(lambda e: run("pe", e))
                block.scalar(lambda e: run("act", e))
                block.vector(lambda e: run("dve", e))
                block.gpsimd(lambda e: run("pool", e))
                block.sync(lambda e: run("sp", e))


class Buf:
    def __init__(self, arena_ap, off, cols, dt):
        self.off, self.cols, self.dt = off, cols, dt
        self.esz = 2 if dt == BF16 else 4
        e0 = off // 2
        if dt == BF16:
            self.ap = arena_ap[:, e0:e0 + cols]
        else:
            self.ap = arena_ap[:, e0:e0 + 2 * cols].bitcast(F32)

    def rg(self, c0=0, c1=None):
        if c1 is None:
            c1 = self.cols
        b0 = self.off + c0 * self.esz
        b1 = self.off + c1 * self.esz
        return [("a", g) for g in range(b0 // GR, (b1 + GR - 1) // GR)]


ARENA_BYTES = 204 * 1024


def build_program(debug=False, stop=99):
    nc = bass.Bass("TRN2", target_bir_lowering=False)
    st = ExitStack()
    P = Prog(nc)

    def dram(name, shape, kind="ExternalInput", dt=F32):
        return nc.dram_tensor(name, list(shape), dt, kind=kind).ap()

    xT_seq = dram("xT_seq", [D, S])
    xT_own = dram("xT_own", [D, T])
    x_own = dram("x_own", [T, D])
    c_col = dram("c_col", [128, KC])
    w_ada = dram("w_ada", [48 * 128, 4096])
    b_ada = dram("b_ada", [1, 6 * D])
    wq = dram("wq", [D, 1024])
    wk = dram("wk", [D, 1024])
    wv = dram("wv", [D, 1024])
    wf_r = dram("wf_r", [128, 256])
    wcq = dram("wcq", [D, 512])
    wckr = dram("wckr", [D, 384])
    wgf = dram("wgf", [D, D])
    wgm = dram("wgm", [D, D])
    bfg = dram("bfg", [4, 4])
    gq = dram("gq", [128, 4])
    gkv = dram("gkv", [128, 2])
    wqn = dram("wqn", [512, 1024])
    wqr = dram("wqr", [512, 1024])
    wkn = dram("wkn", [256, 1024])
    wvm = dram("wvm", [256, 1024])
    wbf = dram("wbf", [1024, D])
    wbm = dram("wbm", [1024, D])
    wout = dram("wout", [D, D])
    lnp = dram("lnp", [4, D])
    lnc = dram("lnc", [128, 64])
    wup = dram("wup", [D, 4 * D])
    wdn = dram("wdn", [4 * D, D])
    ropeK = dram("ropeK", [128, S])
    ropeQ = dram("ropeQ", [128, T])
    maskF = dram("maskF", [128, 4 * 512])
    maskM = dram("maskM", [128, 4 * 512])
    selw = dram("selw", [128, 4])
    ident_d = dram("ident", [128, 128])
    out_d = dram("out", [T, D], kind="ExternalOutput")

    arena_t = st.enter_context(nc.sbuf_tensor("arena", [128, ARENA_BYTES // 2], BF16))
    arena_ap = arena_t[:, :]
    top = [0]

    def alloc(cols, dt):
        esz = 2 if dt == BF16 else 4
        off = top[0]
        top[0] += (cols * esz + GR - 1) // GR * GR
        assert top[0] <= ARENA_BYTES, ("arena overflow", top[0])
        return Buf(arena_ap, off, cols, dt)

    PSW = st.enter_context(nc.psum_tensor("psw", [128, 4096], F32))
    PS = [PSW[:, i * 512:(i + 1) * 512] for i in range(8)]

    def psr(b):
        return [("ps", b)]

    def act(out, in_, func, R, W, bias=None, scale=None):
        kw = {}
        if bias is not None:
            kw["bias"] = bias
        if scale is not None:
            kw["scale"] = scale
        return P.op("act", lambda e: e.activation(out=out, in_=in_, func=func, **kw), R=R, W=W)

    def ts(eng, out, in0, s1, s2, op0, op1, R, W):
        if s2 is None:
            return P.op(eng, lambda e: e.tensor_scalar(out=out, in0=in0, scalar1=s1, scalar2=None, op0=op0), R=R, W=W)
        return P.op(eng, lambda e: e.tensor_scalar(out=out, in0=in0, scalar1=s1, scalar2=s2, op0=op0, op1=op1), R=R, W=W)

    def tt(eng, out, in0, in1, op, R, W):
        return P.op(eng, lambda e: e.tensor_tensor(out=out, in0=in0, in1=in1, op=op), R=R, W=W)

    def stt(eng, out, in0, scalar, in1, op0, op1, R, W):
        return P.op(eng, lambda e: e.scalar_tensor_tensor(out=out, in0=in0, scalar=scalar, in1=in1, op0=op0, op1=op1), R=R, W=W)

    def cp(eng, out, in_, R, W):
        if eng == "act":
            return act(out, in_, AF.Copy, R, W)
        return P.op(eng, lambda e: e.tensor_copy(out=out, in_=in_), R=R, W=W)

    def memset(eng, ap, val, W):
        return P.op(eng, lambda e: e.memset(ap, val), W=W)

    def dma(q, out, in_, R, W, dkey, join=False):
        return P.op(q, lambda e: e.dma_start(out=out, in_=in_), R=R, W=W, dkey=dkey, join=join)

    def mm(out_ap, pairs, R, W, start=True, skip=True):
        def fn(e):
            n = len(pairs)
            ins = None
            for q, (l, r) in enumerate(pairs):
                ins = e.matmul(out_ap, lhsT=l, rhs=r, start=(start and q == 0), stop=(q == n - 1),
                               skip_group_check=skip)
            return ins
        return P.op("pe", fn, R=R, W=W)

    def early(k):
        if stop != k:
            return False
        import os as _os
        npad = int(_os.environ.get("PADMM", "0"))
        for q in range(npad):
            bq = q % 8
            mm(PS[bq][:, 0:128], [(ident.ap[:, :], ident.ap[:, :])], ident.rg(), psr(bq))
        dma("sp", out_d[0:128, 0:128], identf.ap, identf.rg(), [], "out")
        P.emit(st, ["out"])
        st.close()
        return True

    ident = alloc(128, BF16)
    identf = alloc(128, F32)
    ones_bf = alloc(128, BF16)
    ones_f = alloc(128, F32)
    mF = alloc(2048, BF16)
    mM = alloc(2048, BF16)
    selw_sb = alloc(4, F32)
    bfg_raw = alloc(4, F32)
    bfg_sb = alloc(4, F32)
    gq_sb = alloc(4, F32)
    gkv_sb = alloc(2, F32)
    ccol = alloc(KC, F32)
    scT = alloc(KC, BF16)
    modT = alloc(96, F32)
    lnc_sb = alloc(64, F32)
    g2c = alloc(16, F32)
    b2c = alloc(16, F32)
    wfs = alloc(256, BF16)
    NW = 3
    wring = [alloc(4096, BF16) for _ in range(NW)]
    wctr = [0]

    def wslot():
        i = wctr[0] % len(wring)
        wctr[0] += 1
        return i

    def wload(src_ap, nk, ncols, slot=None, col0=0):
        i = wslot() if slot is None else slot
        assert col0 + nk * ncols <= 4096
        wb = wring[i]
        view = wb.ap[:, col0:col0 + nk * ncols].rearrange("p (k n) -> p k n", k=nk)
        srcv = src_ap.rearrange("(k p) n -> p k n", p=128)
        reg = wb.rg(col0, col0 + nk * ncols)
        dma("pool", view, srcv, [], reg, ("w", id(wb)), join=(col0 != 0))
        return view, reg

    dma("sp", ccol.ap, c_col, [], ccol.rg(), "k0")
    dma("sp", selw_sb.ap, selw, [], selw_sb.rg(), "k1")
    dma("sp", bfg_raw.ap[0:4, :], bfg, [], bfg_raw.rg(), "k2")
    ts("dve", bfg_sb.ap[0:4, :], bfg_raw.ap[0:4, :], -1.0, None, ALU.mult, None, bfg_raw.rg(), bfg_sb.rg())
    dma("sp", gq_sb.ap, gq, [], gq_sb.rg(), "k3")
    dma("sp", gkv_sb.ap, gkv, [], gkv_sb.rg(), "k4")
    dma("sp", identf.ap, ident_d, [], identf.rg(), "k5")
    dma("sp", lnc_sb.ap, lnc, [], lnc_sb.rg(), "k10")
    dma("pool", ident.ap, ident_d, [], ident.rg(), "k6")
    dma("pool", mF.ap, maskF, [], mF.rg(), "k7")
    dma("pool", mM.ap, maskM, [], mM.rg(), "k8")
    dma("pool", wfs.ap, wf_r, [], wfs.rg(), "k9")
    memset("dve", ones_bf.ap, 1.0, ones_bf.rg())
    memset("dve", ones_f.ap, 1.0, ones_f.rg())

    psc = [0]

    def nextps():
        i = psc[0] % 8
        psc[0] += 1
        return i

    if early(0):
        return nc
    mark0 = top[0]
    brow = [alloc(256, F32) for _ in range(2)]
    mrow = [alloc(256, F32) for _ in range(2)]
    act(scT.ap, ccol.ap, AF.Silu, ccol.rg(), scT.rg())
    def ada_slab(s, brow, mrow, loader):
        wv_, wreg = loader(w_ada[s * 128:(s + 1) * 128, :])
        br, mr = brow[s % 2], mrow[s % 2]
        dma("sp", br.ap[0:1, :], b_ada[:, s * 256:(s + 1) * 256], [], br.rg(), ("brow", s % 2))
        b = nextps_fn[0]()
        mm(PS[b][0:1, 0:256], [(scT.ap[:, k:k + 1], wv_[:, k, :]) for k in range(KC)],
           scT.rg() + wreg, psr(b))
        tt("dve", mr.ap[0:1, :], PS[b][0:1, 0:256], br.ap[0:1, :], ALU.add, psr(b) + br.rg(), mr.rg())
        b2 = nextps_fn[0]()

        def fn(e, b2=b2, mr=mr):
            ins = None
            for m in range(2):
                ins = e.matmul(PS[b2][:, m:m + 1], lhsT=mr.ap[0:1, m * 128:(m + 1) * 128], rhs=ones_f.ap[0:1, 0:1],
                               start=(m == 0), stop=(m == 1), skip_group_check=True)
            return ins
        P.op("pe", fn, R=mr.rg() + ones_f.rg(), W=psr(b2))
        grp = s // 8
        if grp in (1, 4):
            ts("dve", modT.ap[:, 2 * s:2 * s + 2], PS[b2][:, 0:2], 1.0, None, ALU.add, None, psr(b2), modT.rg(2 * s, 2 * s + 2))
        else:
            cp("dve", modT.ap[:, 2 * s:2 * s + 2], PS[b2][:, 0:2], psr(b2), modT.rg(2 * s, 2 * s + 2))

    def wload_tiled(src2d):
        i = wslot()
        wb = wring[i]
        dma("pool", wb.ap[:, 0:4096], src2d, [], wb.rg(), ("w", id(wb)))
        return wb.ap[:, 0:4096].rearrange("p (k n) -> p k n", k=KC), wb.rg()

    nextps_fn = [nextps]
    for s in range(16):
        ada_slab(s, brow, mrow, wload_tiled)
    top[0] = mark0
    SH1, SC1, G1, SH2, SC2, G2 = 0, 16, 32, 48, 64, 80

    if early(1):
        return nc

    def bcast_rows(dst, c0, eng_copy="dve"):
        dg = alloc(512, F32)
        for kq in range(4):
            for kk in range(4):
                k = kq * 4 + kk
                ts("dve", dg.ap[:, kk * 128:(kk + 1) * 128], identf.ap, modT.ap[:, c0 + k:c0 + k + 1], None, ALU.mult, None,
                   identf.rg() + modT.rg(c0 + k, c0 + k + 1), dg.rg(kk * 128, (kk + 1) * 128))
            b = nextps()
            mm(PS[b][:, :], [(ones_f.ap[:, :], dg.ap[:, :])], ones_f.rg() + dg.rg(), psr(b))
            cp(eng_copy, dst.ap[:, kq * 512:(kq + 1) * 512], PS[b][:, :], psr(b), dst.rg(kq * 512, (kq + 1) * 512))

    def make_stream(nx=3, nu=2):
        xs = [alloc(1024, F32) for _ in range(nx)]
        uts = [alloc(KC * 512, BF16) for _ in range(nu)]
        return dict(xs=xs, uts=uts, xc=0, uc=0)

    def stream_uT(stq, src, t, dst=None, dcol=0, eng="dve"):
        if dst is None:
            ub = stq["uts"][stq["uc"] % len(stq["uts"])]
            stq["uc"] += 1
            base = 0
            stride = 512
        else:
            ub, base, stride = dst, dcol, 1024
        for kg in range(8):
            xb = stq["xs"][stq["xc"] % len(stq["xs"])]
            stq["xc"] += 1
            srcv = src[kg * 256:(kg + 1) * 256, t * 512:(t + 1) * 512].rearrange("(k p) n -> p k n", p=128)
            dma("sp", xb.ap.rearrange("p (k n) -> p k n", k=2), srcv, [], xb.rg(), ("xs", id(xb)))
            for kk in range(2):
                k = 2 * kg + kk
                c0 = k * stride + base
                ts(eng, ub.ap[:, c0:c0 + 512], xb.ap[:, kk * 512:(kk + 1) * 512],
                   modT.ap[:, SC1 + k:SC1 + k + 1], modT.ap[:, SH1 + k:SH1 + k + 1], ALU.mult, ALU.add,
                   xb.rg(kk * 512, (kk + 1) * 512) + modT.rg(SC1 + k, SC1 + k + 1) + modT.rg(SH1 + k, SH1 + k + 1),
                   ub.rg(c0, c0 + 512))
        return ub, base, stride

    def uT_ap(ub, base, stride, k, c0=0, n=512):
        o = k * stride + base + c0
        return ub.ap[:, o:o + n]

    def uT_rg(ub, base, stride, k, c0=0, n=512):
        o = k * stride + base + c0
        return ub.rg(o, o + n)

    PT = [alloc(1024, BF16) for _ in range(2)]
    ptc = [0]
    yb = alloc(8 * 2048, BF16)
    rec = [alloc(8, F32) for _ in range(2)]
    mark_att = top[0]

    SPAIRS = [0, 2]
    OBANKS = [4, 5]
    BBANKS = [6, 7, 0, 1, 2, 3]
    bb_n = [6]
    bctr = [0]

    def nextb():
        i = BBANKS[bctr[0] % bb_n[0]]
        bctr[0] += 1
        return i

    sctr = [0]

    def attention(nh, hd, scale, mask, k_ops, v_ap, v_rg, ycol0, after_slot=None, build_units=None):
        W = nh * 128
        G = 1024 // W
        vw = hd + 1
        pending_units = build_units(0) if build_units is not None else []
        pend = None

        def flush(pend):
            pt, g0, i0, ob0, nsteps0 = pend
            for r in range(G):
                s0 = g0 * G + r
                for hh in range(nh):
                    c0 = r * W + hh * 128

                    def fn(e, pt=pt, s0=s0, hh=hh, ob=ob0, nsteps=nsteps0, c0=c0):
                        return e.matmul(PS[ob][:, hh * vw:(hh + 1) * vw], lhsT=pt.ap[:, c0:c0 + 128],
                                        rhs=v_ap(hh, s0), start=(s0 == 0 and hh == 0), stop=(s0 == nsteps - 1),
                                        skip_group_check=True)
                    P.op("pe", fn, R=pt.rg(c0, c0 + 128) + v_rg(hh, s0), W=psr(ob0))
            if (g0 + 1) * G == nsteps0:
                rc = rec[i0 % 2]
                ov = PS[ob0][:, 0:nh * vw].rearrange("p (h d) -> p h d", d=vw)
                P.op("dve", lambda e, rc=rc, ov=ov: e.reciprocal(out=rc.ap[:, 0:nh], in_=ov[:, :, hd]), R=psr(ob0), W=rc.rg())
                for hh in range(nh):
                    c0 = i0 * 2048 + ycol0 + hh * hd
                    ts("dve", yb.ap[:, c0:c0 + hd], PS[ob0][:, hh * vw:hh * vw + hd], rc.ap[:, hh:hh + 1], None, ALU.mult, None,
                       psr(ob0) + rc.rg(), yb.rg(c0, c0 + hd))
                if after_slot is not None:
                    after_slot(i0)

        for i in range(8):
            for u in pending_units:
                u()
            pending_units = build_units(i + 1) if (build_units is not None and i + 1 < 8) else []
            nsteps = 4 * i + 4
            ngr = nsteps // G
            ob = OBANKS[i % 2]
            for g in range(ngr):
                sp0 = SPAIRS[sctr[0] % 2]
                sctr[0] += 1
                for r in range(G):
                    s = g * G + r
                    col = r * W
                    bk = sp0 + col // 512
                    cb = col % 512
                    first = True
                    if s >= 4 * i:
                        mm(PS[bk][:, cb:cb + W], [(ident.ap[:, :], mask.ap[:, (s - 4 * i) * 512:(s - 4 * i) * 512 + W])],
                           ident.rg() + mask.rg(), psr(bk), start=True)
                        first = False
                    for hh in range(nh):
                        pairs, regs = k_ops(hh, i, s)

                        def fn(e, pairs=pairs, first=first, bk=bk, c0=cb + hh * 128):
                            ins = None
                            for q, (l, r_) in enumerate(pairs):
                                ins = e.matmul(PS[bk][:, c0:c0 + 128], lhsT=l, rhs=r_,
                                               start=(first and q == 0), stop=(q == len(pairs) - 1), skip_group_check=True)
                            return ins
                        P.op("pe", fn, R=regs, W=psr(bk))
                        first = False
                pt = PT[ptc[0] % 2]
                ptc[0] += 1
                act(pt.ap[:, 0:1024], PSW[:, sp0 * 512:sp0 * 512 + 1024], AF.Exp, psr(sp0) + psr(sp0 + 1), pt.rg(0, 1024), scale=scale)
                if pending_units and g >= 1:
                    pending_units.pop(0)()
                if pend is not None:
                    flush(pend)
                pend = (pt, g, i, ob, nsteps)
        flush(pend)

    stq = make_stream()
    KT = alloc(4 * 4096, BF16)
    VA = alloc(32 * 4 * 65, BF16)
    QT = alloc(4 * 1024, BF16)
    fe = alloc(512, F32)
    fl = alloc(512, F32)
    fc = [alloc(512, F32) for _ in range(2)]
    fr1 = alloc(512, F32)
    pk = alloc(3 * 512, BF16)
    co = [alloc(128, F32) for _ in range(2)]
    qr1 = alloc(128, F32)
    pq = alloc(3 * 128, BF16)
    onesr = alloc(512, F32)
    memset("dve", onesr.ap[0:4, :], 1.0, onesr.rg())

    memset("pool", KT.ap[64:70, :], 1.0, KT.rg())
    memset("pool", QT.ap[64:70, :], 1.0, QT.rg())
    memset("pool", VA.ap[:, :], 1.0, VA.rg())
    pre_own = None
    for p in range(4):
        Wq, Wq_r = wload(wq[:, p * 256:(p + 1) * 256], KC, 256)
        Wk, Wk_r = wload(wk[:, p * 256:(p + 1) * 256], KC, 256)
        Wv, Wv_r = wload(wv[:, p * 256:(p + 1) * 256], KC, 256)
        for n in range(2):
            ub, base, stride = pre_own[n] if pre_own is not None else stream_uT(stq, xT_own, n)
            for m in range(2):
                b = nextb()
                mm(PS[b][:, :], [(Wq[:, k, m * 128:(m + 1) * 128], uT_ap(ub, base, stride, k)) for k in range(KC)],
                   Wq_r + ub.rg(), psr(b))
                for hq in range(2):
                    c0 = (2 * m + hq) * 1024 + n * 512
                    cp("act", QT.ap[0:64, c0:c0 + 512], PS[b][hq * 64:(hq + 1) * 64, :], psr(b), QT.rg(c0, c0 + 512))
        nxt = stream_uT(stq, xT_seq, 0)
        for t in range(8):
            ub, base, stride = nxt
            if t + 1 < 8:
                nxt = stream_uT(stq, xT_seq, t + 1)
            for m in range(2):
                b = nextb()
                mm(PS[b][:, :], [(Wk[:, k, m * 128:(m + 1) * 128], uT_ap(ub, base, stride, k)) for k in range(KC)],
                   Wk_r + ub.rg(), psr(b))
                for hq in range(2):
                    c0 = (2 * m + hq) * 4096 + t * 512
                    cp("act", KT.ap[0:64, c0:c0 + 512], PS[b][hq * 64:(hq + 1) * 64, :], psr(b), KT.rg(c0, c0 + 512))
            for sub in range(4):
                b = nextb()
                mm(PS[b][:, 0:256], [(uT_ap(ub, base, stride, k, sub * 128, 128), Wv[:, k, :]) for k in range(KC)],
                   Wv_r + ub.rg(), psr(b))
                c0 = (4 * t + sub) * 260
                ov = VA.ap[:, c0:c0 + 260].rearrange("p (h d) -> p h d", d=65)[:, :, 0:64]
                iv = PS[b][:, 0:256].rearrange("p (h d) -> p h d", d=64)
                cp("act", ov, iv, psr(b), VA.rg(c0, c0 + 260))
            b = nextb()
            mm(PS[b][0:4, :], [(wfs.ap[:, k * 16 + 4 * p:k * 16 + 4 * p + 4], uT_ap(ub, base, stride, k)) for k in range(KC)],
               wfs.rg() + ub.rg(), psr(b))
            act(fe.ap[0:4, :], PS[b][0:4, :], AF.Exp, psr(b) + bfg_sb.rg(), fe.rg(), bias=bfg_sb.ap[0:4, p:p + 1], scale=-1.0)
            act(fl.ap[0:4, :], fe.ap[0:4, :], AF.Ln, fe.rg(), fl.rg(), bias=1.0)
            cc, cprev = fc[t % 2], fc[(t + 1) % 2]
            if t == 0:
                P.op("dve", lambda e, cc=cc: e.tensor_tensor_scan(out=cc.ap[0:4, :], data0=onesr.ap[0:4, :], data1=fl.ap[0:4, :],
                                                                  initial=0.0, op0=ALU.mult, op1=ALU.add),
                     R=onesr.rg() + fl.rg(), W=cc.rg())
            else:
                P.op("dve", lambda e, cc=cc, cprev=cprev: e.tensor_tensor_scan(
                    out=cc.ap[0:4, :], data0=onesr.ap[0:4, :], data1=fl.ap[0:4, :],
                    initial=cprev.ap[0:4, 511:512], op0=ALU.mult, op1=ALU.add),
                    R=onesr.rg() + fl.rg() + cprev.rg(), W=cc.rg())
            ts("dve", pk.ap[0:4, 0:512], cc.ap[0:4, :], 8.0, None, ALU.mult, None, cc.rg(), pk.rg(0, 512))
            stt("dve", fr1.ap[0:4, :], cc.ap[0:4, :], 8.0, pk.ap[0:4, 0:512], ALU.mult, ALU.subtract, cc.rg() + pk.rg(0, 512), fr1.rg())
            cp("dve", pk.ap[0:4, 512:1024], fr1.ap[0:4, :], fr1.rg(), pk.rg(512, 1024))
            tt("dve", pk.ap[0:4, 1024:1536], fr1.ap[0:4, :], pk.ap[0:4, 512:1024], ALU.subtract, fr1.rg() + pk.rg(512, 1024), pk.rg(1024, 1536))
            for hh in range(4):
                for pc in range(3):
                    c0 = hh * 4096 + t * 512
                    dma("pool", KT.ap[64 + pc:65 + pc, c0:c0 + 512], pk.ap[hh:hh + 1, pc * 512:(pc + 1) * 512],
                        pk.rg(pc * 512, (pc + 1) * 512), KT.rg(c0, c0 + 512) + [("augk_t", t)], ("augk", t), join=True)
            ca, cb = co[0], co[1]
            ts("dve", ca.ap[0:4, :], cc.ap[0:4, 0:128], selw_sb.ap[0:4, 0:1], None, ALU.mult, None, cc.rg() + selw_sb.rg(), ca.rg())
            src_, dst_ = ca, cb
            for s_ in range(1, 4):
                stt("dve", dst_.ap[0:4, :], cc.ap[0:4, s_ * 128:(s_ + 1) * 128], selw_sb.ap[0:4, s_:s_ + 1], src_.ap[0:4, :],
                    ALU.mult, ALU.add, cc.rg() + selw_sb.rg() + src_.rg(), dst_.rg())
                src_, dst_ = dst_, src_
            cown = src_
            ts("dve", pq.ap[0:4, 0:128], cown.ap[0:4, :], -8.0, None, ALU.mult, None, cown.rg(), pq.rg(0, 128))
            stt("dve", qr1.ap[0:4, :], cown.ap[0:4, :], -8.0, pq.ap[0:4, 0:128], ALU.mult, ALU.subtract, cown.rg() + pq.rg(0, 128), qr1.rg())
            cp("dve", pq.ap[0:4, 128:256], qr1.ap[0:4, :], qr1.rg(), pq.rg(128, 256))
            tt("dve", pq.ap[0:4, 256:384], qr1.ap[0:4, :], pq.ap[0:4, 128:256], ALU.subtract, qr1.rg() + pq.rg(128, 256), pq.rg(256, 384))
            for hh in range(4):
                for pc in range(3):
                    c0 = hh * 1024 + t * 128
                    dma("pool", QT.ap[67 + pc:68 + pc, c0:c0 + 128], pq.ap[hh:hh + 1, pc * 128:(pc + 1) * 128],
                        pq.rg(pc * 128, (pc + 1) * 128), QT.rg(c0, c0 + 128) + [("augq_t", t)], ("augq", t), join=True)

        if p == 0 and early(2):
            return nc

        def k_ops(hh, i, s):
            kc0 = hh * 4096 + s * 128
            qc0 = hh * 1024 + i * 128
            return ([(KT.ap[0:70, kc0:kc0 + 128], QT.ap[0:70, qc0:qc0 + 128])],
                    KT.rg(kc0, kc0 + 128) + QT.rg(qc0, qc0 + 128) + [("augk_t", s // 4), ("augq_t", i)])

        def v_ap(hh, s):
            c0 = s * 260 + hh * 65
            return VA.ap[:, c0:c0 + 65]

        def v_rg(hh, s):
            c0 = s * 260 + hh * 65
            return VA.rg(c0, c0 + 65)

        nxt_own = []

        def own_hook(i, nxt_own=nxt_own, p=p):
            if p < 3 and i in (5, 6):
                nxt_own.append(stream_uT(stq, xT_own, i - 5))
        attention(4, 64, 0.125, mF, k_ops, v_ap, v_rg, p * 256, after_slot=own_hook)
        pre_own = nxt_own if p < 3 else None
        if p == 0 and early(3):
            return nc

    if early(4):
        return nc
    top[0] = mark_att
    CK = alloc(2 * 4096, BF16)
    KR = alloc(4096, BF16)
    CQ = alloc(4 * 1024, BF16)
    RQ = alloc(1024, F32)
    RTQ = alloc(1024, F32)
    mark_mla = top[0]
    stq = make_stream(nx=6)
    sq = alloc(4 * 512, BF16)
    ckraw = alloc(2 * 512, BF16)
    rk = alloc(512, F32)
    rt = [alloc(512, F32) for _ in range(2)]
    t1 = alloc(512, F32)
    t2 = alloc(512, F32)
    dma("sp", RTQ.ap, ropeQ, [], RTQ.rg(), "rtq")
    memset("pool", KR.ap[64:128, :], 0.0, KR.rg())

    Wcq0, Wcq0_r = wload(wcq[:, 0:256], KC, 256)
    Wcq1, Wcq1_r = wload(wcq[:, 256:512], KC, 256)
    for n in range(2):
        ub, base, stride = stream_uT(stq, xT_own, n)
        for m in range(4):
            Wc, Wc_r = (Wcq0, Wcq0_r) if m < 2 else (Wcq1, Wcq1_r)
            b = nextb()
            mm(PS[b][:, :], [(Wc[:, k, (m % 2) * 128:(m % 2 + 1) * 128], uT_ap(ub, base, stride, k)) for k in range(KC)],
               Wc_r + ub.rg(), psr(b))
            c0 = m * 1024 + n * 512
            cp("act", CQ.ap[:, c0:c0 + 512], PS[b][:, :], psr(b), CQ.rg(c0, c0 + 512))
            act(sq.ap[:, m * 512:(m + 1) * 512], PS[b][:, :], AF.Square, psr(b), sq.rg(m * 512, (m + 1) * 512))
        b = nextb()
        mm(PS[b][:, :], [(ones_bf.ap[:, :], sq.ap[:, m * 512:(m + 1) * 512]) for m in range(4)], ones_bf.rg() + sq.rg(), psr(b))
        ts("dve", rk.ap, PS[b][:, :], 1.0 / 512, RMS_EPS, ALU.mult, ALU.add, psr(b), rk.rg())
        act(rk.ap, rk.ap, AF.Ln, rk.rg(), rk.rg())
        act(RQ.ap[:, n * 512:(n + 1) * 512], rk.ap, AF.Exp, rk.rg(), RQ.rg(n * 512, (n + 1) * 512), scale=-0.5)

    Wckv, Wckv_r = wload(wckr[:, 0:256], KC, 256)
    Wkr, Wkr_r = wload(wckr[:, 256:384], KC, 128)
    nxt = stream_uT(stq, xT_seq, 0)
    for t in range(8):
        ub, base, stride = nxt
        if t + 1 < 8:
            nxt = stream_uT(stq, xT_seq, t + 1)
        rtb = rt[t % 2]
        dma("sp", rtb.ap, ropeK[:, t * 512:(t + 1) * 512], [], rtb.rg(), ("rt", t % 2))
        for m in range(2):
            b = nextb()
            mm(PS[b][:, :], [(Wckv[:, k, m * 128:(m + 1) * 128], uT_ap(ub, base, stride, k)) for k in range(KC)],
               Wckv_r + ub.rg(), psr(b))
            cp("act", ckraw.ap[:, m * 512:(m + 1) * 512], PS[b][:, :], psr(b), ckraw.rg(m * 512, (m + 1) * 512))
            act(sq.ap[:, m * 512:(m + 1) * 512], PS[b][:, :], AF.Square, psr(b), sq.rg(m * 512, (m + 1) * 512))
        b = nextb()
        mm(PS[b][:, :], [(ones_bf.ap[:, :], sq.ap[:, m * 512:(m + 1) * 512]) for m in range(2)], ones_bf.rg() + sq.rg(0, 1024), psr(b))
        ts("dve", t1.ap, PS[b][:, :], 1.0 / 256, RMS_EPS, ALU.mult, ALU.add, psr(b), t1.rg())
        act(t1.ap, t1.ap, AF.Ln, t1.rg(), t1.rg())
        act(rk.ap, t1.ap, AF.Exp, t1.rg(), rk.rg(), scale=-0.5)
        for m in range(2):
            c0 = m * 4096 + t * 512
            tt("dve", CK.ap[:, c0:c0 + 512], ckraw.ap[:, m * 512:(m + 1) * 512], rk.ap, ALU.mult,
               ckraw.rg(m * 512, (m + 1) * 512) + rk.rg(), CK.rg(c0, c0 + 512))
        b = nextb()
        mm(PS[b][:, :], [(Wkr[:, k, :], uT_ap(ub, base, stride, k)) for k in range(KC)], Wkr_r + ub.rg(), psr(b))
        tt("dve", t1.ap[0:64, :], PS[b][0:64, :], rtb.ap[0:64, :], ALU.mult, psr(b) + rtb.rg(), t1.rg())
        tt("dve", t2.ap[0:64, :], PS[b][64:128, :], rtb.ap[64:128, :], ALU.mult, psr(b) + rtb.rg(), t2.rg())
        tt("dve", KR.ap[0:64, t * 512:(t + 1) * 512], t1.ap[0:64, :], t2.ap[0:64, :], ALU.add, t1.rg() + t2.rg(), KR.rg(t * 512, (t + 1) * 512))

    top[0] = mark_mla
    KN = alloc(2 * 4096, BF16)
    VM = alloc(32 * 2 * 129, BF16)
    QN = alloc(2 * 1024, BF16)
    QR = alloc(2 * 1024, BF16)
    WSETS = [(alloc(512, BF16), alloc(512, BF16), alloc(1024, BF16), alloc(1024, BF16)) for _ in range(2)]
    t1 = alloc(512, F32)
    t2 = alloc(512, F32)
    t3 = alloc(512, F32)
    abrow = [alloc(256, F32) for _ in range(2)]
    amrow = [alloc(256, F32) for _ in range(2)]
    aring = [alloc(4096, BF16) for _ in range(2)]
    actr = [16]

    def aload(src):
        wb = aring[actr[0] % 2]
        view = wb.ap[:, 0:4096].rearrange("p (k n) -> p k n", k=KC)
        dma("pool", wb.ap[:, 0:4096], src, [], wb.rg(), ("aw", actr[0] % 2))
        return view, wb.rg()

    def ada_tail(i):
        if actr[0] < 48:
            nextps_fn[0] = nextb
            ada_slab(actr[0], abrow, amrow, aload)
            nextps_fn[0] = nextps
            actr[0] += 1

    memset("pool", QR.ap[64:128, :], 0.0, QR.rg())
    memset("pool", VM.ap[:, :], 1.0, VM.rg())
    bb_n[0] = 2
    def prep_weights(sp_):
        WKN, WVM, WQN, WQR = WSETS[sp_ % 2]
        c0w = sp_ * 256
        wa, wa_r = wload(wkn[:, c0w:c0w + 256], 2, 256)
        wb_, wb_r = wload(wvm[:, c0w:c0w + 256], 2, 256)
        for m in range(2):
            ts("dve", WKN.ap[:, m * 256:(m + 1) * 256], wa[:, m, :], gkv_sb.ap[:, m:m + 1], None, ALU.mult, None,
               wa_r + gkv_sb.rg(), WKN.rg(m * 256, (m + 1) * 256))
            ts("dve", WVM.ap[:, m * 256:(m + 1) * 256], wb_[:, m, :], gkv_sb.ap[:, m:m + 1], None, ALU.mult, None,
               wb_r + gkv_sb.rg(), WVM.rg(m * 256, (m + 1) * 256))
        wc, wc_r = wload(wqn[:, c0w:c0w + 256], 4, 256)
        wd, wd_r = wload(wqr[:, c0w:c0w + 256], 4, 256)
        for m in range(4):
            ts("dve", WQN.ap[:, m * 256:(m + 1) * 256], wc[:, m, :], gq_sb.ap[:, m:m + 1], None, ALU.mult, None,
               wc_r + gq_sb.rg(), WQN.rg(m * 256, (m + 1) * 256))
            ts("dve", WQR.ap[:, m * 256:(m + 1) * 256], wd[:, m, :], gq_sb.ap[:, m:m + 1], None, ALU.mult, None,
               wd_r + gq_sb.rg(), WQR.rg(m * 256, (m + 1) * 256))

    prep_weights(0)
    for sp_ in range(4):
        WKN, WVM, WQN, WQR = WSETS[sp_ % 2]

        def build_units(i, WKN=WKN, WVM=WVM, WQN=WQN, WQR=WQR):
            units = []
            tq = i

            def kn_unit(hh):
                b = nextb()
                mm(PS[b][:, :], [(WKN.ap[:, m * 256 + hh * 128:m * 256 + (hh + 1) * 128], CK.ap[:, m * 4096 + tq * 512:m * 4096 + (tq + 1) * 512])
                                 for m in range(2)],
                   WKN.rg() + CK.rg(tq * 512, (tq + 1) * 512) + CK.rg(4096 + tq * 512, 4096 + (tq + 1) * 512), psr(b))
                c0 = hh * 4096 + tq * 512
                cp("dve", KN.ap[:, c0:c0 + 512], PS[b][:, :], psr(b), KN.rg(c0, c0 + 512))

            def v_unit(tile):
                b = nextb()
                mm(PS[b][:, 0:256], [(CK.ap[:, m * 4096 + tile * 128:m * 4096 + (tile + 1) * 128], WVM.ap[:, m * 256:(m + 1) * 256]) for m in range(2)],
                   WVM.rg() + CK.rg(tile * 128, (tile + 1) * 128) + CK.rg(4096 + tile * 128, 4096 + (tile + 1) * 128), psr(b))
                c0 = tile * 258
                ov = VM.ap[:, c0:c0 + 258].rearrange("p (h d) -> p h d", d=129)[:, :, 0:128]
                iv = PS[b][:, 0:256].rearrange("p (h d) -> p h d", d=128)
                cp("dve", ov, iv, psr(b), VM.rg(c0, c0 + 258))

            def q_unit(n, hh):
                b = nextb()
                mm(PS[b][:, :], [(WQN.ap[:, m * 256 + hh * 128:m * 256 + (hh + 1) * 128], CQ.ap[:, m * 1024 + n * 512:m * 1024 + (n + 1) * 512])
                                 for m in range(4)], WQN.rg() + CQ.rg(), psr(b))
                c0 = hh * 1024 + n * 512
                tt("dve", QN.ap[:, c0:c0 + 512], PS[b][:, :], RQ.ap[:, n * 512:(n + 1) * 512], ALU.mult,
                   psr(b) + RQ.rg(n * 512, (n + 1) * 512), QN.rg(c0, c0 + 512))
                b = nextb()
                mm(PS[b][:, :], [(WQR.ap[:, m * 256 + hh * 128:m * 256 + (hh + 1) * 128], CQ.ap[:, m * 1024 + n * 512:m * 1024 + (n + 1) * 512])
                                 for m in range(4)], WQR.rg() + CQ.rg(), psr(b))
                tt("dve", t1.ap[0:64, :], PS[b][0:64, :], RTQ.ap[0:64, n * 512:(n + 1) * 512], ALU.mult, psr(b) + RTQ.rg(), t1.rg())
                tt("dve", t2.ap[0:64, :], PS[b][64:128, :], RTQ.ap[64:128, n * 512:(n + 1) * 512], ALU.mult, psr(b) + RTQ.rg(), t2.rg())
                tt("dve", t3.ap[0:64, :], t1.ap[0:64, :], t2.ap[0:64, :], ALU.add, t1.rg() + t2.rg(), t3.rg())
                tt("dve", QR.ap[0:64, c0:c0 + 512], t3.ap[0:64, :], RQ.ap[0:64, n * 512:(n + 1) * 512], ALU.mult,
                   t3.rg() + RQ.rg(n * 512, (n + 1) * 512), QR.rg(c0, c0 + 512))

            if i % 4 == 0:
                for hh in range(2):
                    units.append(lambda hh=hh: q_unit(i // 4, hh))
            for hh in range(2):
                units.append(lambda hh=hh: kn_unit(hh))
            for tile in range(4 * i, 4 * i + 4):
                units.append(lambda tile=tile: v_unit(tile))
            return units

        def k_ops(hh, i, s):
            kc0 = hh * 4096 + s * 128
            qc0 = hh * 1024 + i * 128
            return ([(KN.ap[:, kc0:kc0 + 128], QN.ap[:, qc0:qc0 + 128]),
                     (KR.ap[:, s * 128:(s + 1) * 128], QR.ap[:, qc0:qc0 + 128])],
                    KN.rg(kc0, kc0 + 128) + QN.rg(qc0, qc0 + 128) + KR.rg(s * 128, (s + 1) * 128) + QR.rg(qc0, qc0 + 128))

        def v_ap(hh, s):
            c0 = s * 258 + hh * 129
            return VM.ap[:, c0:c0 + 129]

        def v_rg(hh, s):
            c0 = s * 258 + hh * 129
            return VM.rg(c0, c0 + 129)

        def mla_hook(i, sp_=sp_):
            ada_tail(i)
            if i == 3 and sp_ < 3:
                prep_weights(sp_ + 1)
        attention(2, 128, 192.0 ** -0.5, mM, k_ops, v_ap, v_rg, 1024 + sp_ * 256, after_slot=mla_hook, build_units=build_units)

    if early(5):
        return nc
    dbg_keys = []
    if debug:
        for nm, bf_, dt_ in (("KN", KN, BF16), ("KR", KR, BF16), ("QN", QN, BF16), ("QR", QR, BF16), ("VM", VM, BF16),
                            ("CK", CK, BF16), ("CQ", CQ, BF16), ("RQ", RQ, F32)):
            dd = dram("dbg_" + nm, [128, bf_.cols], kind="ExternalOutput", dt=dt_)
            dma("sp", dd, bf_.ap, bf_.rg(), [], "dbgx" + nm)
            dbg_keys.append("dbgx" + nm)
        d_yb = dram("dbg_yb", [128, 16384], kind="ExternalOutput", dt=BF16)
        dma("sp", d_yb, yb.ap, yb.rg(), [], "dbg0")
        dbg_keys.append("dbg0")
    top[0] = mark_att
    yT = alloc(16 * 1024, BF16)
    UO = alloc(16 * 1024, BF16)
    MT = alloc(16 * 1024, BF16)
    mark_mg = top[0]
    stq = make_stream(nx=3, nu=0)
    sA = alloc(512, F32)
    sB = alloc(512, F32)
    m1 = alloc(512, F32)
    m2 = alloc(512, F32)
    for i in range(8):
        for cq in range(4):
            b = nextps()
            pv = PS[b]

            def fn(e, pv=pv, i=i, cq=cq):
                ins = None
                for c4 in range(4):
                    c = cq * 4 + c4
                    ins = e.matmul(pv[:, c4 * 128:(c4 + 1) * 128], lhsT=yb.ap[:, i * 2048 + c * 128:i * 2048 + (c + 1) * 128],
                                   rhs=ident.ap[:, :], start=(c4 == 0), stop=(c4 == 3), skip_group_check=True)
                return ins
            P.op("pe", fn, R=yb.rg(i * 2048 + cq * 512, i * 2048 + (cq + 1) * 512) + ident.rg(), W=psr(b))
            for c4 in range(4):
                c = cq * 4 + c4
                c0 = c * 1024 + i * 128
                cp("act" if cq % 2 == 0 else "dve", yT.ap[:, c0:c0 + 128], pv[:, c4 * 128:(c4 + 1) * 128], psr(b), yT.rg(c0, c0 + 128))
    if early(51):
        return nc
    for n in range(2):
        stream_uT(stq, xT_own, n, dst=UO, dcol=n * 512)
    if early(52):
        return nc
    extra = [Buf(arena_ap, yb.off + q * 8192, 4096, BF16) for q in range(4)]
    wring.extend(extra)
    for m in range(16):
        if m == 1 and early(53):
            return nc
        s1 = wslot()
        Wgf_, Wgf_r = wload(wgf[:, m * 128:(m + 1) * 128], KC, 128, slot=s1, col0=0)
        Wgm_, Wgm_r = wload(wgm[:, m * 128:(m + 1) * 128], KC, 128, slot=s1, col0=2048)
        s2 = wslot()
        Wbf_, Wbf_r = wload(wbf[:, m * 128:(m + 1) * 128], 8, 128, slot=s2, col0=0)
        Wbm_, Wbm_r = wload(wbm[:, m * 128:(m + 1) * 128], 8, 128, slot=s2, col0=1024)
        for n in range(2):
            bA, bB, bPA, bPB = nextps(), nextps(), nextps(), nextps()
            mm(PS[bPA][:, :], [(Wbf_[:, k, :], yT.ap[:, k * 1024 + n * 512:k * 1024 + (n + 1) * 512]) for k in range(8)], wring[s2].rg() + yT.rg(0, 8192), psr(bPA))
            mm(PS[bPB][:, :], [(Wbm_[:, k, :], yT.ap[:, (8 + k) * 1024 + n * 512:(8 + k) * 1024 + (n + 1) * 512]) for k in range(8)],
               wring[s2].rg() + yT.rg(8192, 16384), psr(bPB))
            mm(PS[bA][:, :], [(Wgf_[:, k, :], UO.ap[:, k * 1024 + n * 512:k * 1024 + (n + 1) * 512]) for k in range(KC)], wring[s1].rg() + UO.rg(), psr(bA))
            mm(PS[bB][:, :], [(Wgm_[:, k, :], UO.ap[:, k * 1024 + n * 512:k * 1024 + (n + 1) * 512]) for k in range(KC)], wring[s1].rg() + UO.rg(), psr(bB))
            act(sA.ap, PS[bA][:, :], AF.Sigmoid, psr(bA), sA.rg())
            act(sB.ap, PS[bB][:, :], AF.Sigmoid, psr(bB), sB.rg())
            tt("dve", m1.ap, PS[bPA][:, :], sA.ap, ALU.mult, psr(bPA) + sA.rg(), m1.rg())
            tt("dve", m2.ap, PS[bPB][:, :], sB.ap, ALU.mult, psr(bPB) + sB.rg(), m2.rg())
            c0 = m * 1024 + n * 512
            tt("dve", MT.ap[:, c0:c0 + 512], m1.ap, m2.ap, ALU.add, m1.rg() + m2.rg(), MT.rg(c0, c0 + 512))

    if early(6):
        return nc
    if debug:
        d_mt = dram("dbg_mt", [128, 16384], kind="ExternalOutput", dt=BF16)
        dma("sp", d_mt, MT.ap, MT.rg(), [], "dbg1")
        dbg_keys.append("dbg1")
    ACC = Buf(arena_ap, yb.off, 8 * 2048, F32)
    assert yb.off + 8 * 2048 * 4 <= UO.off
    U2T = UO
    _MT = MT
    for q in range(4):
        wring.pop()
    extra = [Buf(arena_ap, UO.off + q * 8192, 4096, BF16) for q in range(4)]
    wring.extend(extra)
    top[0] = mark_mg
    G1B = alloc(2048, BF16)
    bcast_rows(G1B, G1)
    xo = [alloc(512, F32) for _ in range(4)]
    xoc = [0]
    for j in range(4):
        sa_ = wslot()
        Wa, Wa_r = wload(wout[0:1024, j * 512:(j + 1) * 512], 8, 512, slot=sa_)
        sb2 = wslot()
        Wb, Wb_r = wload(wout[1024:2048, j * 512:(j + 1) * 512], 8, 512, slot=sb2)
        for (Wx, Wx_r, sl) in ((Wa, Wa_r, sa_), (Wb, Wb_r, sb2)):
            for k in range(8):
                tt("pool", Wx[:, k, :], Wx[:, k, :], G1B.ap[:, j * 512:(j + 1) * 512], ALU.mult,
                   wring[sl].rg(k * 512, (k + 1) * 512) + G1B.rg(j * 512, (j + 1) * 512), wring[sl].rg(k * 512, (k + 1) * 512))
        for tq in range(8):
            xb = xo[xoc[0] % 4]
            xoc[0] += 1
            dma("sp", xb.ap, x_own[tq * 128:(tq + 1) * 128, j * 512:(j + 1) * 512], [], xb.rg(), ("xo", id(xb)))
            b = nextps()
            mm(PS[b][:, :], [((_MT.ap[:, k * 1024 + tq * 128:k * 1024 + (tq + 1) * 128]), (Wa if k < 8 else Wb)[:, k % 8, :]) for k in range(KC)],
               _MT.rg() + wring[sa_].rg() + wring[sb2].rg(), psr(b))
            c0 = tq * 2048 + j * 512
            stt("dve", ACC.ap[:, c0:c0 + 512], xb.ap, ALPHA, PS[b][:, :], ALU.mult, ALU.add, xb.rg() + psr(b), ACC.rg(c0, c0 + 512))

    if debug:
        d_a1 = dram("dbg_acc1", [128, 16384], kind="ExternalOutput")
        dma("sp", d_a1, ACC.ap, ACC.rg(), [], "dbg2")
        dbg_keys.append("dbg2")
    for q in range(4):
        wring.pop()
    top[0] = mark_mg
    print('mark_mg', mark_mg) if DEBUG_SRC else None
    stats = alloc(4 * 6, F32)
    mv = alloc(2, F32)
    rstd = alloc(1, F32)
    tmpv = alloc(1, F32)

    def layernorm_stats(src_ap_fn, src_rg, stats, mv, rstd, tmpv):
        for c in range(4):
            P.op("dve", lambda e, c=c, o_=stats.ap[:, c * 6:(c + 1) * 6], i_=src_ap_fn(c): e.bn_stats(out=o_, in_=i_),
                 R=src_rg, W=stats.rg(c * 6, (c + 1) * 6))
        P.op("dve", lambda e, o_=mv.ap[:, 0:2], i_=stats.ap[:, 0:24]: e.bn_aggr(out=o_, in_=i_), R=stats.rg(), W=mv.rg())
        ts("dve", tmpv.ap, mv.ap[:, 1:2], LN_EPS, None, ALU.add, None, mv.rg(), tmpv.rg())
        act(tmpv.ap, tmpv.ap, AF.Ln, tmpv.rg(), tmpv.rg())
        act(rstd.ap, tmpv.ap, AF.Exp, tmpv.rg(), rstd.rg(), scale=-0.5)

    LNG = Buf(arena_ap, MT.off, 2048, F32)
    LNB = Buf(arena_ap, MT.off + 8192, 2048, F32)
    dma("sp", LNG.ap, lnp[0:1, :].broadcast_to([128, D]), [], LNG.rg(), "lng")
    dma("sp", LNB.ap, lnp[1:2, :].broadcast_to([128, D]), [], LNB.rg(), "lnb")
    tt("dve", g2c.ap, lnc_sb.ap[:, 0:16], modT.ap[:, SC2:SC2 + 16], ALU.mult, lnc_sb.rg() + modT.rg(SC2, SC2 + 16), g2c.rg())
    tt("dve", b2c.ap, lnc_sb.ap[:, 16:32], modT.ap[:, SC2:SC2 + 16], ALU.mult, lnc_sb.rg() + modT.rg(SC2, SC2 + 16), b2c.rg())
    tt("dve", b2c.ap, b2c.ap, modT.ap[:, SH2:SH2 + 16], ALU.add, b2c.rg() + modT.rg(SH2, SH2 + 16), b2c.rg())
    xns = [alloc(2048, F32) for _ in range(2)]
    xnbs = [alloc(2048, BF16) for _ in range(2)]
    sts = [(stats, mv, rstd, tmpv), (alloc(4 * 6, F32), alloc(2, F32), alloc(1, F32), alloc(1, F32))]

    def ln1_main(tq):
        a0 = tq * 2048
        xn, xnb = xns[tq % 2], xnbs[tq % 2]
        st_, mv_, rs_, tv_ = sts[tq % 2]
        layernorm_stats(lambda c, a0=a0: ACC.ap[:, a0 + c * 512:a0 + (c + 1) * 512], ACC.rg(a0, a0 + 2048), st_, mv_, rs_, tv_)
        ts("dve", xn.ap, ACC.ap[:, a0:a0 + 2048], mv_.ap[:, 0:1], rs_.ap[:, 0:1], ALU.subtract, ALU.mult,
           ACC.rg(a0, a0 + 2048) + mv_.rg() + rs_.rg(), xn.rg())
        cp("act", xnb.ap, xn.ap, xn.rg(), xnb.rg())
        for cq in range(4):
            b = nextps()
            pv = PS[b]

            def fn(e, pv=pv, cq=cq, xnb=xnb):
                ins = None
                for c4 in range(4):
                    c = cq * 4 + c4
                    ins = e.matmul(pv[:, c4 * 128:(c4 + 1) * 128], lhsT=xnb.ap[:, c * 128:(c + 1) * 128],
                                   rhs=ident.ap[:, :], start=(c4 == 0), stop=(c4 == 3), skip_group_check=True)
                return ins
            P.op("pe", fn, R=xnb.rg(cq * 512, (cq + 1) * 512) + ident.rg(), W=psr(b))
            for c4 in range(4):
                c = cq * 4 + c4
                c0 = c * 1024 + tq * 128
                if cq % 2 == 0:
                    act(U2T.ap[:, c0:c0 + 128], pv[:, c4 * 128:(c4 + 1) * 128], AF.Identity, psr(b) + g2c.rg() + b2c.rg(), U2T.rg(c0, c0 + 128),
                        bias=b2c.ap[:, c:c + 1], scale=g2c.ap[:, c:c + 1])
                else:
                    ts("dve", U2T.ap[:, c0:c0 + 128], pv[:, c4 * 128:(c4 + 1) * 128], g2c.ap[:, c:c + 1], b2c.ap[:, c:c + 1], ALU.mult, ALU.add,
                       psr(b) + g2c.rg() + b2c.rg(), U2T.rg(c0, c0 + 128))

    def ln1_side(tq):
        a0 = tq * 2048
        xn = xns[tq % 2]
        tt("pool", xn.ap, xn.ap, LNG.ap, ALU.mult, xn.rg() + LNG.rg(), xn.rg())
        tt("pool", xn.ap, xn.ap, LNB.ap, ALU.add, xn.rg() + LNB.rg(), xn.rg())
        act(ACC.ap[:, a0:a0 + 2048], xn.ap, AF.Copy, xn.rg(), ACC.rg(a0, a0 + 2048), scale=ALPHA)

    for tq in range(8):
        ln1_main(tq)
        if tq >= 1:
            ln1_side(tq - 1)
    ln1_side(7)

    if early(7):
        return nc
    top[0] = MT.off
    G2B = alloc(2048, BF16)
    bcast_rows(G2B, G2)
    HR = [alloc(512, BF16) for _ in range(2)]
    HT = [alloc(4 * 1024, BF16) for _ in range(2)]
    xr = [alloc(4096, BF16) for _ in range(3)]
    wring.extend(xr)
    hrc = [0]
    for f in range(16):
        su0, su1 = wslot(), wslot()
        Wu0, Wu0_r = wload(wup[0:1024, f * 512:(f + 1) * 512], 8, 512, slot=su0)
        Wu1, Wu1_r = wload(wup[1024:2048, f * 512:(f + 1) * 512], 8, 512, slot=su1)
        sd0, sd1 = wslot(), wslot()
        Wd0, Wd0_r = wload(wdn[f * 512:f * 512 + 256, :], 2, 2048, slot=sd0)
        Wd1, Wd1_r = wload(wdn[f * 512 + 256:(f + 1) * 512, :], 2, 2048, slot=sd1)
        for (Wx, sl) in ((Wd0, sd0), (Wd1, sd1)):
            for k in range(2):
                tt("pool", Wx[:, k, :], Wx[:, k, :], G2B.ap, ALU.mult,
                   wring[sl].rg(k * 2048, (k + 1) * 2048) + G2B.rg(), wring[sl].rg(k * 2048, (k + 1) * 2048))
        ht = HT[f % 2]
        for hc in range(4):
            for n in range(2):
                b = nextps()
                mm(PS[b][:, :], [((Wu0 if k < 8 else Wu1)[:, k % 8, hc * 128:(hc + 1) * 128], U2T.ap[:, k * 1024 + n * 512:k * 1024 + (n + 1) * 512])
                                 for k in range(KC)], wring[su0].rg() + wring[su1].rg() + U2T.rg(), psr(b))
                hr = HR[hrc[0] % 2]
                hrc[0] += 1
                act(hr.ap, PS[b][:, :], AF.Relu, psr(b), hr.rg())
                c0 = hc * 1024 + n * 512
                tt("pool", ht.ap[:, c0:c0 + 512], hr.ap, hr.ap, ALU.mult, hr.rg(), ht.rg(c0, c0 + 512))
        for tq in range(8):
            for j in range(4):
                b = nextps()
                mm(PS[b][:, :], [(ht.ap[:, hc * 1024 + tq * 128:hc * 1024 + (tq + 1) * 128], (Wd0 if hc < 2 else Wd1)[:, hc % 2, j * 512:(j + 1) * 512])
                                 for hc in range(4)], ht.rg() + wring[sd0].rg() + wring[sd1].rg(), psr(b))
                c0 = tq * 2048 + j * 512
                tt("dve", ACC.ap[:, c0:c0 + 512], ACC.ap[:, c0:c0 + 512], PS[b][:, :], ALU.add, ACC.rg(c0, c0 + 512) + psr(b), ACC.rg(c0, c0 + 512))

    if early(8):
        return nc
    if debug:
        d_a2 = dram("dbg_acc2", [128, 16384], kind="ExternalOutput")
        dma("sp", d_a2, ACC.ap, ACC.rg(), [], "dbg3")
        dbg_keys.append("dbg3")
    top[0] = MT.off
    LNG2 = alloc(2048, F32)
    LNB2 = alloc(2048, F32)
    dma("sp", LNG2.ap, lnp[2:3, :].broadcast_to([128, D]), [], LNG2.rg(), "lng2")
    dma("sp", LNB2.ap, lnp[3:4, :].broadcast_to([128, D]), [], LNB2.rg(), "lnb2")
    stats = alloc(4 * 6, F32)
    mv = alloc(2, F32)
    rstd = alloc(1, F32)
    tmpv = alloc(1, F32)
    ob_ = [alloc(2048, F32) for _ in range(2)]
    for tq in range(8):
        a0 = tq * 2048
        o_ = ob_[tq % 2]
        layernorm_stats(lambda c, a0=a0: ACC.ap[:, a0 + c * 512:a0 + (c + 1) * 512], ACC.rg(a0, a0 + 2048), stats, mv, rstd, tmpv)
        ts("dve", o_.ap, ACC.ap[:, a0:a0 + 2048], mv.ap[:, 0:1], rstd.ap[:, 0:1], ALU.subtract, ALU.mult,
           ACC.rg(a0, a0 + 2048) + mv.rg() + rstd.rg(), o_.rg())
        tt("pool", o_.ap, o_.ap, LNG2.ap, ALU.mult, o_.rg() + LNG2.rg(), o_.rg())
        tt("dve", o_.ap, o_.ap, LNB2.ap, ALU.add, o_.rg() + LNB2.rg(), o_.rg())
        dma("sp", out_d[tq * 128:(tq + 1) * 128, :], o_.ap, o_.rg(), [], "out")

    P.emit(st, ["out"] + dbg_keys)
    st.close()
    return nc


def _prep(inputs):
    f = lambda a: np.ascontiguousarray(np.asarray(a, dtype=np.float32))
    x = f(inputs["x"])
    c = f(inputs["c"])
    w_in = f(inputs["w_in"])[0]
    cuts = np.cumsum([3072, 16, 512, 256, 64, 2048, 2048])
    qkv = w_in[:, :3072]
    wf_ = w_in[:, 3072:3088]
    wcq_ = w_in[:, 3088:3600]
    wckv_ = w_in[:, 3600:3856]
    wkr_ = w_in[:, 3856:3920]
    wgf_ = w_in[:, 3920:5968]
    wgm_ = w_in[:, 5968:8016]
    wkr_sw = np.concatenate([wkr_[:, 32:], wkr_[:, :32]], axis=1)
    wqup = f(inputs["w_q_up"])[0].reshape(512, 8, 192)
    wqn_ = np.ascontiguousarray(wqup[:, :, :128].reshape(512, 1024))
    qr = wqup[:, :, 128:]
    wqr_ = np.ascontiguousarray(np.concatenate([qr, qr[:, :, 32:], qr[:, :, :32]], axis=2).reshape(512, 1024))
    wkvup = f(inputs["w_kv_up"])[0].reshape(256, 8, 256)
    wkn_ = np.ascontiguousarray(wkvup[:, :, :128].reshape(256, 1024))
    wvm_ = np.ascontiguousarray(wkvup[:, :, 128:].reshape(256, 1024))
    bfv = f(inputs["b_forget"])[0]
    shared = {
        "w_ada": f(f(inputs["w_ada"])[0].reshape(16, 128, 48, 256).transpose(2, 1, 0, 3).reshape(48 * 128, 4096)),
        "b_ada": f(inputs["b_ada"])[0][None, :],
        "wq": f(qkv[:, 0:1024]), "wk": f(qkv[:, 1024:2048]), "wv": f(qkv[:, 2048:3072]),
        "wf_r": f(wf_.reshape(16, 128, 16).transpose(1, 0, 2).reshape(128, 256)),
        "wcq": f(wcq_), "wckr": f(np.concatenate([wckv_, wkr_, wkr_sw], axis=1)),
        "wgf": f(wgf_), "wgm": f(wgm_),
        "bfg": f(bfv.reshape(4, 4).T),
        "gq": f(f(inputs["g_q_norm"])[0].reshape(4, 128).T),
        "gkv": f(f(inputs["g_kv_norm"])[0].reshape(2, 128).T),
        "wqn": wqn_, "wqr": wqr_, "wkn": wkn_, "wvm": wvm_,
        "wbf": f(inputs["w_branch_fox"])[0], "wbm": f(inputs["w_branch_mla"])[0],
        "wout": f(inputs["w_out"])[0],
        "lnp": f(np.stack([f(inputs["ln1_g"])[0], f(inputs["ln1_b"])[0], f(inputs["ln2_g"])[0], f(inputs["ln2_b"])[0]])),
        "wup": f(inputs["w_mlp_up"])[0], "wdn": f(inputs["w_mlp_down"])[0],
        "lnc": f(np.concatenate([f(inputs[k_])[0].reshape(16, 128).T for k_ in ("ln1_g", "ln1_b", "ln2_g", "ln2_b")], axis=1)),
        "ident": np.eye(128, dtype=np.float32),
    }
    pos = np.arange(S, dtype=np.float32)
    inv = (10000.0 ** (-np.arange(0, 64, 2, dtype=np.float32) / 64)).astype(np.float32)
    ang = (pos[:, None] * inv[None, :]).astype(np.float32)
    cos, sin = np.cos(ang).T.astype(np.float32), np.sin(ang).T.astype(np.float32)
    ropeK = f(np.concatenate([cos, cos, -sin, sin], axis=0))
    kk = np.arange(128)[:, None]
    qq = np.arange(128)[None, :]
    tri = np.where(kk <= qq, 0.0, NEG).astype(np.float32)
    chk = np.where((kk // 64) <= (qq // 64), 0.0, NEG).astype(np.float32)
    in_maps = []
    for cidx in range(8):
        b, j = cidx // 4, cidx % 4
        blocks = [4 * i + j for i in range(8)]
        tok = np.concatenate([np.arange(g * 128, (g + 1) * 128) for g in blocks])
        xb = x[b]
        mF_ = np.zeros((128, 4, 4, 128), np.float32)
        mM_ = np.zeros((128, 4, 4, 128), np.float32)
        for s in range(4):
            if s == j:
                mF_[:, s] = tri[:, None, :]
                mM_[:, s] = chk[:, None, :]
            elif s > j:
                mF_[:, s] = NEG
                mM_[:, s] = NEG
        sel = np.zeros((128, 4), np.float32)
        sel[:, j] = 1.0
        m = dict(shared)
        m.update({
            "xT_seq": f(xb.T), "xT_own": f(xb[tok].T), "x_own": f(xb[tok]),
            "c_col": f(c[b].reshape(16, 128).T),
            "ropeK": ropeK, "ropeQ": f(ropeK[:, tok]),
            "maskF": f(mF_.reshape(128, 2048)), "maskM": f(mM_.reshape(128, 2048)),
            "selw": sel,
        })
        in_maps.append((m, b, tok))
    return in_maps


_NC_CACHE = {}


def kernel(**inputs):
    prepped = _prep(inputs)
    if "nc" not in _NC_CACHE:
        _NC_CACHE["nc"] = build_program()
    nc = _NC_CACHE["nc"]
    res = run_bass_kernel_spmd(nc, [m for (m, _, _) in prepped], core_ids=list(range(8)))
    out = np.zeros((2, S, D), np.float32)
    for (m, b, tok), r in zip(prepped, res.results):
        out[b, tok, :] = np.asarray(r["out"], dtype=np.float32)
    return out
```
